# Optimizing a Trainium2 kernel written in Bass

```python
import math
import jax, jax.numpy as jnp
from jax import lax
import numpy as np

D_MODEL = 1024
BATCH = 8
SEQ = 4096
DEPTH = 4

EXPAND = 2
MIX_WIDTH = EXPAND * D_MODEL
PLE_DIM = 256
N_A = DEPTH // 2
N_B = DEPTH - N_A
EPS = 1e-6

GLA_HEADS = 4
GLA_KEY_WIDTH = MIX_WIDTH // 2
GLA_DK = GLA_KEY_WIDTH // GLA_HEADS
GLA_DV = MIX_WIDTH // GLA_HEADS
GLA_GATE_RANK = 16
GLA_GATE_NORMALIZER = 16.0
GLA_CHUNK = 64
GLA_IN_COLS = 2 * GLA_KEY_WIDTH + 2 * MIX_WIDTH + GLA_GATE_RANK

DIFF_HEAD_DIM = 128
DIFF_HEADS = MIX_WIDTH // (2 * DIFF_HEAD_DIM)
DIFF_QBLOCK = 128
DIFF_IN_COLS = 2 * MIX_WIDTH
KV_COLS = 2 * MIX_WIDTH

kernel_name = "yoco_gla_diffattn_hybrid"


def _rmsnorm(x, g):
    xf = x.astype(jnp.float32)
    y = xf * lax.rsqrt(jnp.mean(xf * xf, axis=-1, keepdims=True) + EPS)
    return (y * g.astype(jnp.float32)).astype(x.dtype)


def _to_chunks(t):
    b, s, h, d = t.shape
    return t.reshape(b, s // GLA_CHUNK, GLA_CHUNK, h, d).transpose(1, 0, 3, 2, 4)


def _gla_mixer(xn, w_in, w_gk2, b_gk, g_norm, w_out):
    bsz, s, _ = xn.shape
    proj = xn @ w_in
    q, k, v, gate, gk_lr = jnp.split(
        proj, [GLA_KEY_WIDTH, 2 * GLA_KEY_WIDTH, 2 * GLA_KEY_WIDTH + MIX_WIDTH,
               2 * GLA_KEY_WIDTH + 2 * MIX_WIDTH], axis=-1)
    gk = gk_lr @ w_gk2 + b_gk
    log_a = jax.nn.log_sigmoid(gk.astype(jnp.float32)) / GLA_GATE_NORMALIZER
    f32 = jnp.float32
    qc = _to_chunks(q.reshape(bsz, s, GLA_HEADS, GLA_DK).astype(f32) * GLA_DK ** -0.5)
    kc = _to_chunks(k.reshape(bsz, s, GLA_HEADS, GLA_DK).astype(f32))
    vc = _to_chunks(v.reshape(bsz, s, GLA_HEADS, GLA_DV).astype(f32))
    bc = jnp.cumsum(_to_chunks(log_a.reshape(bsz, s, GLA_HEADS, GLA_DK)), axis=3)
    causal = jnp.tril(jnp.ones((GLA_CHUNK, GLA_CHUNK), dtype=bool))

    def step(state, inp):
        qi, ki, vi, bi = inp
        inter = jnp.einsum('bhid,bhde->bhie', qi * jnp.exp(bi), state)
        decay = bi[:, :, :, None, :] - bi[:, :, None, :, :]
        decay = jnp.exp(jnp.where(causal[None, None, :, :, None], decay, -jnp.inf))
        scores = jnp.einsum('bhid,bhijd,bhjd->bhij', qi, decay, ki)
        intra = jnp.einsum('bhij,bhje->bhie', scores, vi)
        b_last = bi[:, :, -1, :]
        k_dec = ki * jnp.exp(b_last[:, :, None, :] - bi)
        new_state = jnp.exp(b_last)[..., None] * state + jnp.einsum('bhjd,bhje->bhde', k_dec, vi)
        return new_state, inter + intra

    s0 = jnp.zeros((bsz, GLA_HEADS, GLA_DK, GLA_DV), f32)
    _, o = lax.scan(step, s0, (qc, kc, vc, bc))
    o = o.transpose(1, 0, 3, 2, 4).reshape(bsz, s, GLA_HEADS, GLA_DV)
    o = _rmsnorm(o, g_norm).reshape(bsz, s, MIX_WIDTH)
    o = (o * jax.nn.silu(gate.astype(f32))).astype(xn.dtype)
    return o @ w_out


def _shared_kv(h, kv_norm, w_kv):
    bsz, s, _ = h.shape
    kv = _rmsnorm(h, kv_norm) @ w_kv
    k, v = jnp.split(kv, [MIX_WIDTH], axis=-1)
    k = k.reshape(bsz, s, DIFF_HEADS, 2, DIFF_HEAD_DIM).transpose(0, 2, 3, 1, 4)
    v = v.reshape(bsz, s, DIFF_HEADS, 2 * DIFF_HEAD_DIM).transpose(0, 2, 1, 3)
    return k[:, :, 0], k[:, :, 1], v


def _diff_mixer(xn, k1, k2, v, w_in, lam, g_norm, w_out, lambda_init):
    bsz, s, _ = xn.shape
    q, gate = jnp.split(xn @ w_in, [MIX_WIDTH], axis=-1)
    q = q.reshape(bsz, s, DIFF_HEADS, 2, DIFF_HEAD_DIM)
    nb = s // DIFF_QBLOCK

    def blocks(t):
        return t.reshape(bsz, nb, DIFF_QBLOCK, DIFF_HEADS, DIFF_HEAD_DIM).transpose(1, 0, 3, 2, 4)

    q1b, q2b = blocks(q[..., 0, :]), blocks(q[..., 1, :])
    lamf = lam.astype(jnp.float32)
    lam_full = (jnp.exp(jnp.sum(lamf[0] * lamf[1])) - jnp.exp(jnp.sum(lamf[2] * lamf[3]))
                + lambda_init)
    scale = DIFF_HEAD_DIM ** -0.5
    kpos = jnp.arange(s, dtype=jnp.int32)
    starts = jnp.arange(nb, dtype=jnp.int32) * DIFF_QBLOCK

    def block(args):
        qb1, qb2, start = args
        mask = kpos[None, :] <= (start + jnp.arange(DIFF_QBLOCK, dtype=jnp.int32))[:, None]
        s1 = jnp.einsum('bhqd,bhkd->bhqk', qb1, k1).astype(jnp.float32) * scale
        s2 = jnp.einsum('bhqd,bhkd->bhqk', qb2, k2).astype(jnp.float32) * scale
        p1 = jax.nn.softmax(jnp.where(mask, s1, -jnp.inf), axis=-1)
        p2 = jax.nn.softmax(jnp.where(mask, s2, -jnp.inf), axis=-1)
        attn = (p1 - lam_full * p2).astype(v.dtype)
        return jnp.einsum('bhqk,bhkd->bhqd', attn, v)

    o = lax.map(block, (q1b, q2b, starts))
    o = o.transpose(1, 0, 3, 2, 4).reshape(bsz, s, DIFF_HEADS, 2 * DIFF_HEAD_DIM)
    o = _rmsnorm(o, g_norm) * (1.0 - lambda_init)
    o = o.reshape(bsz, s, MIX_WIDTH)
    o = (o.astype(jnp.float32) * jax.nn.silu(gate.astype(jnp.float32))).astype(xn.dtype)
    return o @ w_out


def _ple(h, p_i, g_norm, w_gate, w_proj):
    gate = jax.nn.sigmoid((_rmsnorm(h, g_norm) @ w_gate).astype(jnp.float32))
    return (gate * (p_i @ w_proj).astype(jnp.float32)).astype(h.dtype)


def setup_inputs(seed: int = 0) -> dict:
    key = jax.random.key(seed)
    ks = jax.random.split(key, 20)
    f32 = jnp.float32

    def nrm(k, shape, scale):
        return jax.random.normal(k, shape, f32) * scale

    def gain(k, shape):
        return 1.0 + 0.05 * jax.random.normal(k, shape, f32)

    return {
        "x": nrm(ks[0], (BATCH, SEQ, D_MODEL), 1.0),
        "p": nrm(ks[1], (DEPTH, BATCH, SEQ, PLE_DIM), 1.0),
        "norm_mix": gain(ks[2], (DEPTH, D_MODEL)),
        "gla_w_in": nrm(ks[3], (N_A, D_MODEL, GLA_IN_COLS), D_MODEL ** -0.5),
        "gla_w_gk2": nrm(ks[4], (N_A, GLA_GATE_RANK, GLA_KEY_WIDTH), GLA_GATE_RANK ** -0.5),
        "gla_b_gk": nrm(ks[5], (N_A, GLA_KEY_WIDTH), 0.1),
        "gla_norm": gain(ks[6], (N_A, GLA_DV)),
        "gla_w_out": nrm(ks[7], (N_A, MIX_WIDTH, D_MODEL), MIX_WIDTH ** -0.5),
        "kv_norm": gain(ks[8], (D_MODEL,)),
        "w_kv": nrm(ks[9], (D_MODEL, KV_COLS), D_MODEL ** -0.5),
        "diff_w_in": nrm(ks[10], (N_B, D_MODEL, DIFF_IN_COLS), D_MODEL ** -0.5),
        "diff_lambda": nrm(ks[11], (N_B, 4, DIFF_HEAD_DIM), 0.1),
        "diff_norm": gain(ks[12], (N_B, 2 * DIFF_HEAD_DIM)),
        "diff_w_out": nrm(ks[13], (N_B, MIX_WIDTH, D_MODEL), MIX_WIDTH ** -0.5),
        "ple_norm": gain(ks[14], (DEPTH, D_MODEL)),
        "ple_w_gate": nrm(ks[15], (DEPTH, D_MODEL, D_MODEL), D_MODEL ** -0.5),
        "ple_w_proj": nrm(ks[16], (DEPTH, PLE_DIM, D_MODEL), PLE_DIM ** -0.5),
        "final_norm": gain(ks[17], (D_MODEL,)),
    }


def reference(x, p, norm_mix, gla_w_in, gla_w_gk2, gla_b_gk, gla_norm, gla_w_out,
              kv_norm, w_kv, diff_w_in, diff_lambda, diff_norm, diff_w_out,
              ple_norm, ple_w_gate, ple_w_proj, final_norm):
    h = x
    k1 = k2 = v = None
    for i in range(DEPTH):
        hn = _rmsnorm(h, norm_mix[i])
        if i < N_A:
            h = h + _gla_mixer(hn, gla_w_in[i], gla_w_gk2[i], gla_b_gk[i], gla_norm[i], gla_w_out[i])
        else:
            j = i - N_A
            lambda_init = 0.8 - 0.6 * math.exp(-0.3 * i)
            h = h + _diff_mixer(hn, k1, k2, v, diff_w_in[j], diff_lambda[j], diff_norm[j],
                                diff_w_out[j], lambda_init)
        h = h + _ple(h, p[i], ple_norm[i], ple_w_gate[i], ple_w_proj[i])
        if i == N_A - 1:
            k1, k2, v = _shared_kv(h, kv_norm, w_kv)
    return _rmsnorm(h, final_norm)
```

```python
import math
from contextlib import ExitStack

import numpy as np
import concourse.bass as bass
import concourse.mybir as mybir
from concourse.bass_utils import run_bass_kernel_spmd

F32 = mybir.dt.float32
BF16 = mybir.dt.bfloat16
AF = mybir.ActivationFunctionType
ALU = mybir.AluOpType

S = 4096
D = 1024
NCH = S // 128
MIX = 2048
PLE = 256
EPS = 1e-6
GLA_IN = 6160
N_CORES = 8

SAME_ENGINE_SYNC = True


class Buf:
    __slots__ = ("name", "last_write", "reads", "psum")

    def __init__(self, name=""):
        self.name = name
        self.last_write = None
        self.reads = []
        self.psum = False


class TB:
    def __init__(self, t, name):
        self.t = t
        self.b = Buf(name)

    def __getitem__(self, k):
        return self.t[k]


class Eng:
    def __init__(self, name, sem):
        self.name = name
        self.sem = sem
        self.count = 0
        self.ops = []
        self.waited = {}
        self.pending = False


class Prog:
    def __init__(self, nc):
        self.nc = nc
        self.es = ExitStack()
        self.engs = {}
        for n in ("pe", "act", "dve", "pool", "sp"):
            sem = self.es.enter_context(nc.semaphore("sem_" + n))
            self.engs[n] = Eng(n, sem)
        self.dma_sems = {}
        self.n_ops = 0

    def dma_sem(self, key):
        if key not in self.dma_sems:
            sem = self.es.enter_context(self.nc.semaphore("dsem_%d" % len(self.dma_sems)))
            self.dma_sems[key] = [sem, 0]
        return self.dma_sems[key]

    def _deps(self, reads, writes):
        deps = []
        for b in reads:
            if b.last_write is not None:
                deps.append(b.last_write)
            if b.psum:
                deps.extend(b.reads)
        for b in writes:
            if b.last_write is not None:
                deps.append(b.last_write)
            deps.extend(b.reads)
        return deps

    def _emit_waits(self, e, deps):
        need = {}
        for (sem, val) in deps:
            if sem is e.sem:
                if e.name == "pe" or not SAME_ENGINE_SYNC:
                    continue
                if val > e.count:
                    continue
            k = id(sem)
            if e.waited.get(k, 0) >= val:
                continue
            if k not in need or need[k][1] < val:
                need[k] = (sem, val)
        for k, (sem, val) in need.items():
            e.waited[k] = val
            e.ops.append(("wait", sem, val))

    def _check_pending(self, e):
        for o in self.engs.values():
            if o is not e and o.pending:
                raise RuntimeError("op on %s while %s has an unsignalled group open" % (e.name, o.name))

    def op(self, eng, fn, reads=(), writes=(), signal=True):
        e = self.engs[eng]
        self._check_pending(e)
        reads = [r.b if isinstance(r, TB) else r for r in reads]
        writes = [w.b if isinstance(w, TB) else w for w in writes]
        self._emit_waits(e, self._deps(reads, writes))
        self.n_ops += 1
        if signal:
            e.count += 1
            tok = (e.sem, e.count)
            e.ops.append(("op", fn, e.sem, 1))
            e.pending = False
        else:
            tok = (e.sem, e.count + 1)
            e.ops.append(("op", fn, None, 0))
            e.pending = True
        for b in reads:
            b.reads.append(tok)
        for b in writes:
            b.last_write = tok
            b.reads = []
        return tok

    def dma(self, queue, key, out, in_, reads=(), writes=(), **kw):
        e = self.engs[queue]
        self._check_pending(e)
        assert not e.pending
        reads = [r.b if isinstance(r, TB) else r for r in reads]
        writes = [w.b if isinstance(w, TB) else w for w in writes]
        self._emit_waits(e, self._deps(reads, writes))
        ds = self.dma_sem(key)
        ds[1] += 16
        tok = (ds[0], ds[1])
        e.ops.append(("op", lambda h, o=out, i=in_, kw=kw: h.dma_start(out=o, in_=i, **kw), ds[0], 16))
        self.n_ops += 1
        for b in reads:
            b.reads.append(tok)
        for b in writes:
            b.last_write = tok
            b.reads = []
        return tok

    def dma_group(self, queue, key, pairs, reads=(), writes=(), **kw):
        e = self.engs[queue]
        self._check_pending(e)
        reads = [r.b if isinstance(r, TB) else r for r in reads]
        writes = [w.b if isinstance(w, TB) else w for w in writes]
        self._emit_waits(e, self._deps(reads, writes))
        ds = self.dma_sem(key)
        for (out, in_) in pairs:
            ds[1] += 16
            e.ops.append(("op", lambda h, o=out, i=in_, kw=kw: h.dma_start(out=o, in_=i, **kw), ds[0], 16))
            self.n_ops += 1
        tok = (ds[0], ds[1])
        for b in reads:
            b.reads.append(tok)
        for b in writes:
            b.last_write = tok
            b.reads = []
        return tok

    def barrier(self):
        for e in self.engs.values():
            assert not e.pending
        toks = [(e.sem, e.count) for e in self.engs.values() if e.count > 0]
        toks += [(s, v) for (s, v) in self.dma_sems.values() if v > 0]
        for e in self.engs.values():
            self._emit_waits(e, [t for t in toks if t[0] is not e.sem])

    def emit_block(self):
        nc = self.nc
        for e in self.engs.values():
            assert not e.pending, e.name

        def replay(h, e):
            for rec in e.ops:
                if rec[0] == "wait":
                    h.wait_ge(rec[1], rec[2])
                else:
                    ins = rec[1](h)
                    if rec[2] is not None:
                        ins.then_inc(rec[2], rec[3])
            e.ops = []

        with nc.Block() as block:
            @block.tensor
            def _(h):
                replay(h, self.engs["pe"])

            @block.scalar
            def _(h):
                replay(h, self.engs["act"])

            @block.vector
            def _(h):
                replay(h, self.engs["dve"])

            @block.gpsimd
            def _(h):
                replay(h, self.engs["pool"])

            @block.sync
            def _(h):
                replay(h, self.engs["sp"])

    def close(self):
        self.es.close()

    def mm(self, out, lhsT, rhs, start, stop, reads, writes, signal=None):
        if signal is None:
            signal = stop
        return self.op("pe", lambda h: h.matmul(out, lhsT=lhsT, rhs=rhs, start=start, stop=stop),
                       reads, writes, signal=signal)

    def tr(self, out, in_, ident, reads, writes, signal=True):
        return self.op("pe", lambda h: h.transpose(out, in_, ident), reads, writes, signal=signal)

    def act(self, out, in_, func, reads, writes, eng="act", **kw):
        return self.op(eng, lambda h: h.activation(out=out, in_=in_, func=func, **kw), reads, writes)

    def tt(self, eng, out, in0, in1, op, reads, writes):
        return self.op(eng, lambda h: h.tensor_tensor(out=out, in0=in0, in1=in1, op=op), reads, writes)

    def ts(self, eng, out, in0, s1, s2, op0, op1, reads, writes):
        if s2 is None:
            return self.op(eng, lambda h: h.tensor_scalar(out=out, in0=in0, scalar1=s1, scalar2=None, op0=op0),
                           reads, writes)
        return self.op(eng, lambda h: h.tensor_scalar(out=out, in0=in0, scalar1=s1, scalar2=s2, op0=op0, op1=op1),
                       reads, writes)

    def stt(self, out, in0, scalar, in1, op0, op1, reads, writes):
        return self.op("dve", lambda h: h.scalar_tensor_tensor(out=out, in0=in0, scalar=scalar, in1=in1,
                                                                 op0=op0, op1=op1), reads, writes)

    def copy(self, eng, out, in_, reads, writes):
        if eng == "act":
            return self.op("act", lambda h: h.activation(out=out, in_=in_, func=AF.Copy), reads, writes)
        return self.op(eng, lambda h: h.tensor_copy(out=out, in_=in_), reads, writes)

    def memset(self, eng, ap, val, writes):
        return self.op(eng, lambda h: h.memset(ap, val), [], writes)

    def recip(self, out, in_, reads, writes):
        return self.op("dve", lambda h: h.reciprocal(out=out, in_=in_), reads, writes)


class Phase:
    def __init__(self, nc, P, name):
        self.nc, self.P, self.name = nc, P, name
        self.es = ExitStack()
        self.k = 0

    def sb(self, name, shape, dtype):
        self.k += 1
        t = self.es.enter_context(self.nc.sbuf_tensor("%s_%s" % (self.name, name), list(shape), dtype))
        return TB(t, name)

    def ps(self, name, shape, dtype=F32):
        t = self.es.enter_context(self.nc.psum_tensor("%s_%s" % (self.name, name), list(shape), dtype))
        tb = TB(t, name)
        tb.b.psum = True
        return tb

    def end(self):
        self.P.barrier()
        self.P.emit_block()
        self.es.close()


def rstd_from_ms(P, ms_ap, out_ap, n, reads, writes, eng="pool"):
    if eng == "act":
        P.act(out_ap, ms_ap, AF.Ln, reads, writes, scale=1.0 / n, bias=EPS)
        P.act(out_ap, out_ap, AF.Exp, writes, writes, scale=-0.5)
        return
    nh = P.nhalf
    P.ts("pool", out_ap, ms_ap, 1.0 / n, EPS, ALU.mult, ALU.add, reads, writes)
    P.tt("pool", out_ap, out_ap, nh[:, 0:1], ALU.pow, list(writes) + [nh], writes)


def build_program(debug=False, phases=("g0", "b0", "g1", "b1", "d2", "b2", "d3", "b3")):
    nc = bass.Bass("TRN2", target_bir_lowering=False)

    def din(name, shape):
        return nc.dram_tensor(name, list(shape), F32, kind="ExternalInput").ap()

    x = din("x", [S, D])
    p_in = din("p", [4, S, PLE])
    norm_mix = din("norm_mix", [4, D])
    gla_w_in = din("gla_w_in", [2, D, GLA_IN])
    gla_w_gk2 = din("gla_w_gk2", [2, 16, 1024])
    gla_b_gk = din("gla_b_gk", [2, 1024])
    gla_norm = din("gla_norm", [2, 512])
    gla_w_out = din("gla_w_out", [2, MIX, D])
    kv_norm = din("kv_norm", [D])
    w_kv = din("w_kv", [D, 4096])
    diff_w_in = din("diff_w_in", [2, D, 4096])
    diff_lambda = din("diff_lambda", [2, 4, 128])
    diff_norm = din("diff_norm", [2, 256])
    diff_w_out = din("diff_w_out", [2, MIX, D])
    ple_norm = din("ple_norm", [4, D])
    ple_w_gate = din("ple_w_gate", [4, D, D])
    ple_w_proj = din("ple_w_proj", [4, PLE, D])
    final_norm = din("final_norm", [D])

    out = nc.dram_tensor("out", [S, D], F32, kind="ExternalOutput").ap()
    skind = "ExternalOutput" if debug else "Internal"
    hbuf = nc.dram_tensor("hbuf", [S, D], F32, kind=skind).ap()
    on_d = nc.dram_tensor("on_d", [S, MIX], BF16, kind=skind).ap()
    kT_d = nc.dram_tensor("kT_d", [16, 128, S], BF16, kind=skind).ap()
    v_d = nc.dram_tensor("v_d", [S, MIX], BF16, kind=skind).ap()

    P = Prog(nc)
    h_bufs = [Buf("h%d" % c) for c in range(NCH)]
    on_bufs = [Buf("on%d" % c) for c in range(NCH)]
    out_bufs = [Buf("out%d" % c) for c in range(NCH)]
    kv_buf = Buf("kvd")

    G = Phase(nc, P, "c")
    ident = G.sb("ident", [128, 128], BF16)
    mask2 = G.sb("mask2", [128, 4, 128], BF16)
    U32 = G.sb("U32", [128, 128], F32)
    L32 = G.sb("L32", [128, 128], F32)
    ones32 = G.sb("ones32", [128, 128], F32)
    P.memset("pool", ident[:], 0.0, [ident])
    P.op("pool", lambda h: h.affine_select(out=ident[:], in_=ident[:], pattern=[[-1, 128]],
                                           compare_op=ALU.not_equal, fill=1.0, base=0, channel_multiplier=1),
         [ident], [ident])
    P.memset("pool", mask2[:], 1.0, [mask2])
    for r in range(4):
        P.op("pool", lambda h, r=r: h.affine_select(out=mask2[:, r, :], in_=mask2[:, r, :], pattern=[[1, 128]],
                                                    compare_op=ALU.is_ge, fill=0.0, base=0, channel_multiplier=-1),
             [mask2], [mask2])
    P.memset("pool", U32[:], 1.0, [U32])
    P.op("pool", lambda h: h.affine_select(out=U32[:], in_=U32[:], pattern=[[1, 128]],
                                           compare_op=ALU.is_ge, fill=0.0, base=0, channel_multiplier=-1),
         [U32], [U32])
    P.memset("pool", L32[:], 1.0, [L32])
    P.op("pool", lambda h: h.affine_select(out=L32[:], in_=L32[:], pattern=[[-1, 128]],
                                           compare_op=ALU.is_gt, fill=0.0, base=0, channel_multiplier=1),
         [L32], [L32])
    P.memset("pool", ones32[:], 1.0, [ones32])
    L16 = G.sb("L16", [128, 128], BF16)
    P.memset("pool", L16[:], 1.0, [L16])
    P.op("pool", lambda h: h.affine_select(out=L16[:], in_=L16[:], pattern=[[-1, 128]],
                                           compare_op=ALU.is_gt, fill=0.0, base=0, channel_multiplier=1),
         [L16], [L16])
    ones16 = G.sb("ones16", [128, 2], BF16)
    P.memset("pool", ones16[:], 1.0, [ones16])
    maskb = G.sb("maskb", [128, 128], BF16)
    P.memset("pool", maskb[:], -30000.0, [maskb])
    P.op("pool", lambda h: h.affine_select(out=maskb[:], in_=maskb[:], pattern=[[-1, 128]],
                                           compare_op=ALU.is_gt, fill=0.0, base=0, channel_multiplier=1),
         [maskb], [maskb])
    nhalf = G.sb("nhalf", [128, 2], F32)
    P.memset("pool", nhalf[:], -0.5, [nhalf])
    P.nhalf = nhalf

    def run_pipeline(stages, n):
        ns = len(stages)
        for step in range(n + ns - 1):
            for k in range(ns - 1, -1, -1):
                c = step - k
                if 0 <= c < n:
                    stages[k](c)

    def phase_gla(li, h_src):
        Z = Phase(nc, P, "g%d" % li)
        w_in = Z.sb("w_in", [128, 8, GLA_IN], BF16)
        wgk = Z.sb("wgk", [32, 1024], BF16)
        g_mix = Z.sb("g_mix", [128, D], F32)
        gn = Z.sb("gn", [128, 512], F32)
        hx = [Z.sb("hx%d" % i, [128, D], F32) for i in range(2)]
        hn = [Z.sb("hn%d" % i, [128, D], BF16) for i in range(2)]
        hnT = [Z.sb("hnT%d" % i, [128, 8, 128], BF16) for i in range(3)]
        v_sb = Z.sb("v", [128, MIX], BF16)
        gsil = Z.sb("gsil", [128, MIX], BF16)
        gkl = Z.sb("gkl", [128, 32], BF16)
        gklT = Z.sb("gklT", [32, 128], BF16)
        sp = [Z.sb("sp%d" % i, [128, 1024], F32) for i in range(1)] * 2
        sph = Z.sb("sph", [128, 1024], BF16)
        spl = Z.sb("spl", [128, 1024], BF16)
        E = [[Z.sb("E%d_%d" % (i, k), [128, 1024], F32) for k in range(3)] for i in range(2)]
        ebl = [Z.sb("ebl%d" % i, [128, 8], F32) for i in range(2)]
        qt = Z.sb("qt", [128, 1024], BF16)
        kt = Z.sb("kt", [128, 1024], BF16)
        kd = Z.sb("kd", [128, 1024], BF16)
        qtT = Z.sb("qtT", [128, 8, 128], BF16)
        ktT = Z.sb("ktT", [128, 8, 128], BF16)
        scT = Z.sb("scT", [128, 512], BF16)
        S32 = [Z.sb("S32_%d" % i, [128, 512], F32) for i in range(8)]
        S16 = [Z.sb("S16_%d" % i, [128, 512], BF16) for i in range(8)]
        on_sb = [Z.sb("on%d" % i, [128, MIX], BF16) for i in range(1)] * 2
        ms = [Z.sb("ms%d" % i, [128, 2], F32) for i in range(2)]
        rstd = [Z.sb("rstd%d" % i, [128, 2], F32) for i in range(2)]
        mso = Z.sb("mso", [128, 4], F32)
        rso = Z.sb("rso", [128, 4], F32)
        A = [Z.ps("A%d" % i, [128, 512]) for i in range(2)]
        Tb = Z.ps("T", [128, 1024], BF16)
        Gk = [Z.ps("G%d" % i, [128, 512]) for i in range(2)]
        M = Z.ps("M", [128, 512])
        O = [Z.ps("O%d" % i, [128, 512]) for i in range(2)]
        Tf = TB(Tb.t.bitcast(F32), "Tf")
        Tf.b = Tb.b
        O4 = [O[0], O[1], M, Tf]

        wgrp = [("gk", 6144, 6160, 0), ("gate", 4096, 6144, 4), ("v", 2048, 4096, 5), ("q", 0, 1024, 6),
                ("k", 1024, 2048, 7)]
        wB = {}
        for (nm, c0, c1, key) in wgrp:
            wB[nm] = Buf("w_" + nm)

        def wbuf(col0):
            if col0 >= 6144:
                return wB["gk"]
            if col0 >= 4096:
                return wB["gate"]
            if col0 >= 2048:
                return wB["v"]
            return wB["q"] if col0 < 1024 else wB["k"]

        nm, c0, c1, key = wgrp[0]
        P.dma_group("pool", ("w", key), [(w_in[:, k, c0:c1], gla_w_in[li, k * 128:(k + 1) * 128, c0:c1])
                                         for k in range(8)], writes=[wB[nm]])
        P.dma_group("pool", ("w", 1), [(wgk[0:16, :], gla_w_gk2[li]), (wgk[16:17, :], gla_b_gk[li:li + 1, :])],
                    writes=[wgk])
        P.dma("sp", ("w", 2), g_mix[:], norm_mix[li].partition_broadcast(128), writes=[g_mix])
        P.dma("sp", ("w", 3), gn[:], gla_norm[li].partition_broadcast(128), writes=[gn])
        for (nm, c0, c1, key) in wgrp[1:]:
            P.dma_group("pool", ("w", key), [(w_in[:, k, c0:c1], gla_w_in[li, k * 128:(k + 1) * 128, c0:c1])
                                             for k in range(8)], writes=[wB[nm]], max_dma_last_dim=4096)
        for i in range(8):
            P.memset("pool", S32[i][:], 0.0, [S32[i]])
            P.memset("pool", S16[i][:], 0.0, [S16[i]])
        P.memset("pool", gkl[:], 1.0, [gkl])

        a_rot = [0]
        tmp_bufs = [[Buf("tmp%d_%d" % (i, h)) for h in range(4)] for i in range(2)]

        def nextA():
            a_rot[0] ^= 1
            return A[a_rot[0]]

        def proj(hT, col0, n, dst):
            for k in range(8):
                P.mm(dst[:, 0:n], hT[:, k, :], w_in[:, k, col0:col0 + n], k == 0, k == 7, [hT, wbuf(col0)], [dst])

        def s0(c):
            P.dma("sp", ("hx", c % 2), hx[c % 2][:], h_src[c * 128:(c + 1) * 128, :],
                  reads=[h_bufs[c]], writes=[hx[c % 2]])

        def s1(c):
            hc, m, rs, hn_ = hx[c % 2], ms[c % 2], rstd[c % 2], hn[c % 2]
            P.act(hn_[:], hc[:], AF.Square, [hc], [hn_, m], accum_out=m[:, 0:1])
            rstd_from_ms(P, m[:, 0:1], rs[:, 0:1], D, [m], [rs], eng="act")
            P.stt(hn_[:], hc[:], rs[:, 0:1], g_mix[:], ALU.mult, ALU.mult, [hc, rs, g_mix], [hn_])

        def s2(c):
            hn_, hT = hn[c % 2], hnT[c % 3]
            for k in range(8):
                P.tr(Tb[:, k * 128:(k + 1) * 128], hn_[:, k * 128:(k + 1) * 128], ident[:], [hn_, ident], [Tb],
                     signal=(k == 7))
            P.copy("dve", hT[:].rearrange("p k n -> p (k n)"), Tb[:], [Tb], [hT])

        def s3a_1(c):
            hT = hnT[c % 3]
            for k in range(8):
                P.mm(M[:, 0:16], hT[:, k, :], w_in[:, k, 6144:6160], k == 0, k == 7, [hT, wB["gk"]], [M])
            P.copy("act", gkl[:, 0:16], M[:, 0:16], [M], [gkl])

        def s3a_2(c):
            P.tr(Tb[0:32, 0:128], gkl[:], ident[:], [gkl, ident], [Tb])
            P.copy("act", gklT[:], Tb[0:32, 0:128], [Tb], [gklT])

        def s3a_3(c):
            sp_ = sp[c % 2]
            for nb in range(2):
                P.mm(Gk[nb][:], gklT[0:17, :], wgk[0:17, nb * 512:(nb + 1) * 512], True, True, [gklT, wgk], [Gk[nb]])
                P.act(sp_[:, nb * 512:(nb + 1) * 512], Gk[nb][:], AF.Exp, [Gk[nb]], [sp_], scale=-1.0)
            P.act(sp_[:], sp_[:], AF.Ln, [sp_], [sp_], bias=1.0)
            P.copy("act", sph[:], sp_[:], [sp_], [sph])
            P.tt("pool", spl[:], sp_[:], sph[:], ALU.subtract, [sp_, sph], [spl])

        def s3b(c):
            sp_, E0, E1, E2, eb = sp[c % 2], E[c % 2][0], E[c % 2][1], E[c % 2][2], ebl[c % 2]
            U16 = mask2[:, 0, :]
            for nb in range(2):
                P.mm(Gk[nb][:], U16, sph[:, nb * 512:(nb + 1) * 512], True, False, [mask2, sph], [Gk[nb]], signal=False)
                P.mm(Gk[nb][:], U16, spl[:, nb * 512:(nb + 1) * 512], False, True, [mask2, spl], [Gk[nb]])
            for f in range(8):
                P.mm(M[:, 2 * f:2 * f + 2], sph[:, f * 128:(f + 1) * 128], ones16[:, 0:2], True, False,
                     [sph, ones16], [M], signal=False)
                P.mm(M[:, 2 * f:2 * f + 2], spl[:, f * 128:(f + 1) * 128], ones16[:, 0:2], False, True,
                     [spl, ones16], [M], signal=(f == 7))
            for nb in range(2):
                P.act(E0[:, nb * 512:(nb + 1) * 512], Gk[nb][:], AF.Exp, [Gk[nb]], [E0], scale=-1.0 / 16)
                P.act(E1[:, nb * 512:(nb + 1) * 512], Gk[nb][:], AF.Exp, [Gk[nb]], [E1], scale=1.0 / 16)
            P.act(eb[:], M[:, 0:16].rearrange("p (f t) -> p f t", t=2)[:, :, 0], AF.Exp, [M], [eb], scale=-1.0 / 16)
            for nb in range(2):
                P.mm(Gk[nb][:], L16[:], sph[:, nb * 512:(nb + 1) * 512], True, False, [L16, sph], [Gk[nb]], signal=False)
                P.mm(Gk[nb][:], L16[:], spl[:, nb * 512:(nb + 1) * 512], False, True, [L16, spl], [Gk[nb]])
                P.act(E2[:, nb * 512:(nb + 1) * 512], Gk[nb][:], AF.Exp, [Gk[nb]], [E2], scale=-1.0 / 16)

        def s4_q(c):
            hT, E0 = hnT[c % 3], E[c % 2][0]
            for nb in range(2):
                a = nextA()
                proj(hT, nb * 512, 512, a)
                P.stt(qt[:, nb * 512:(nb + 1) * 512], a[:], 256 ** -0.5, E0[:, nb * 512:(nb + 1) * 512],
                      ALU.mult, ALU.mult, [a, E0], [qt])

        def s4_k(c):
            hT, E1, E2 = hnT[c % 3], E[c % 2][1], E[c % 2][2]
            for nb in range(2):
                a = nextA()
                proj(hT, 1024 + nb * 512, 512, a)
                P.tt("dve", kt[:, nb * 512:(nb + 1) * 512], a[:], E1[:, nb * 512:(nb + 1) * 512], ALU.mult,
                     [a, E1], [kt])
                P.tt("dve", kd[:, nb * 512:(nb + 1) * 512], a[:], E2[:, nb * 512:(nb + 1) * 512], ALU.mult,
                     [a, E2], [kd])

        def s4_gate(c):
            hT = hnT[c % 3]
            for nb in range(4):
                a = nextA()
                proj(hT, 4096 + nb * 512, 512, a)
                P.act(gsil[:, nb * 512:(nb + 1) * 512], a[:], AF.Silu, [a], [gsil])
            for nb in range(4):
                P.tt("pool", gsil[:, nb * 512:(nb + 1) * 512], gsil[:, nb * 512:(nb + 1) * 512], gn[:], ALU.mult,
                     [gsil, gn], [gsil])

        def s4_v(c):
            hT = hnT[c % 3]
            for nb in range(4):
                a = nextA()
                proj(hT, 2048 + nb * 512, 512, a)
                P.copy("dve", v_sb[:, nb * 512:(nb + 1) * 512], a[:], [a], [v_sb])

        def s4_tr(c):
            for k in range(8):
                P.tr(Tb[:, k * 128:(k + 1) * 128], qt[:, k * 128:(k + 1) * 128], ident[:], [qt, ident], [Tb],
                     signal=(k == 7))
            P.copy("act", qtT[:].rearrange("p k n -> p (k n)"), Tb[:], [Tb], [qtT])
            for k in range(8):
                P.tr(Tb[:, k * 128:(k + 1) * 128], kt[:, k * 128:(k + 1) * 128], ident[:], [kt, ident], [Tb],
                     signal=(k == 7))
            P.copy("dve", ktT[:].rearrange("p k n -> p (k n)"), Tb[:], [Tb], [ktT])

        def s4_rec(c):
            eb = ebl[c % 2]
            for hh in range(4):
                for dt in range(2):
                    P.mm(M[:, hh * 128:(hh + 1) * 128], ktT[:, 2 * hh + dt, :], qtT[:, 2 * hh + dt, :],
                         dt == 0, dt == 1, [ktT, qtT], [M], signal=(hh == 3 and dt == 1))
            P.tt("dve", scT[:], M[:], mask2[:].rearrange("p r n -> p (r n)"), ALU.mult, [M, mask2], [scT])
            osb = on_sb[c % 2]
            KV = [A[0], A[1], Gk[0], Gk[1]]
            E1, E2 = E[c % 2][1], E[c % 2][2]
            tmpB = tmp_bufs[c % 2]
            for hh in range(4):
                o = O[hh % 2]
                tmpT = E1 if hh < 2 else E2
                tmp = tmpT[:, (hh % 2) * 512:(hh % 2 + 1) * 512]
                for dt in range(2):
                    P.mm(o[:], qtT[:, 2 * hh + dt, :], S16[2 * hh + dt][:], dt == 0, False,
                         [qtT, S16[2 * hh + dt]], [o], signal=False)
                P.mm(o[:], scT[:, hh * 128:(hh + 1) * 128], v_sb[:, hh * 512:(hh + 1) * 512], False, True,
                     [scT, v_sb], [o])
                P.tt("dve", tmp, o[:], gsil[:, hh * 512:(hh + 1) * 512], ALU.mult, [o, gsil], [tmpB[hh]])
                P.act(kt[:, 0:512], o[:], AF.Square, [o], [kt, mso], accum_out=mso[:, hh:hh + 1])
                rstd_from_ms(P, mso[:, hh:hh + 1], rso[:, hh:hh + 1], 512, [mso], [rso], eng="act")
                for dt in range(2):
                    i = 2 * hh + dt
                    a = KV[i % 4]
                    P.mm(a[:], kd[:, i * 128:(i + 1) * 128], v_sb[:, hh * 512:(hh + 1) * 512], True, True,
                         [kd, v_sb], [a])
                    P.stt(S32[i][:], S32[i][:], eb[:, i:i + 1], a[:], ALU.mult, ALU.add, [S32[i], eb, a], [S32[i]])
                    P.copy("pool", S16[i][:], S32[i][:], [S32[i]], [S16[i]])
                P.act(osb[:, hh * 512:(hh + 1) * 512], tmp, AF.Copy, [tmpB[hh], tmpT, rso], [osb],
                      scale=rso[:, hh:hh + 1])
            P.dma("pool", ("on", c % 2), on_d[c * 128:(c + 1) * 128, :], osb[:], reads=[osb], writes=[on_bufs[c]])

        for step in range(NCH + 4):
            c3, c4 = step - 3, step - 4
            v3, v4 = 0 <= c3 < NCH, 0 <= c4 < NCH
            if v3:
                s3a_1(c3)
            if v4:
                s4_gate(c4)
            if v3:
                s3a_2(c3)
            if v4:
                s4_v(c4)
            if v3:
                s3a_3(c3)
            if v4:
                s4_q(c4)
                s4_k(c4)
            if v3:
                s3b(c3)
            if v4:
                s4_tr(c4)
            c2 = step - 2
            if 0 <= c2 < NCH:
                s2(c2)
            if v4:
                s4_rec(c4)
            for k, fn in ((1, s1), (0, s0)):
                c = step - k
                if 0 <= c < NCH:
                    fn(c)
        Z.end()

    def phase_b(li, h_src, w_out_d, final):
        Z = Phase(nc, P, "b%d" % li)
        w_out = Z.sb("w_out", [128, 16, D], BF16)
        wg = Z.sb("wg", [128, 8, D], BF16)
        wp = Z.sb("wp", [128, 2, D], BF16)
        g_ple = Z.sb("g_ple", [128, D], F32)
        hx = [Z.sb("hx%d" % i, [128, D], F32) for i in range(2)]
        on_sb = [Z.sb("on%d" % i, [128, MIX], BF16) for i in range(2)]
        p_sb = [Z.sb("p%d" % i, [128, PLE], BF16) for i in range(2)]
        onTa = [Z.sb("onTa%d" % i, [128, 8, 128], BF16) for i in range(2)]
        onTb = [Z.sb("onTb%d" % i, [128, 8, 128], BF16) for i in range(2)]
        pT = [Z.sb("pT%d" % i, [128, 2, 128], BF16) for i in range(2)]
        junk = Z.sb("junk", [128, D], BF16)
        h1 = [Z.sb("h1_%d" % i, [128, D], F32) for i in range(3)]
        hn2 = [Z.sb("hn2_%d" % i, [128, D], BF16) for i in range(2)]
        hn2T = [Z.sb("hn2T%d" % i, [128, 8, 128], BF16) for i in range(2)]
        sg = Z.sb("sg", [128, D], F32)
        h2 = [Z.sb("h2_%d" % i, [128, D], F32) for i in range(2)]
        ms = [Z.sb("ms%d" % i, [128, 2], F32) for i in range(4)]
        rstd = [Z.sb("rstd%d" % i, [128, 2], F32) for i in range(4)]
        T0 = Z.ps("T0", [128, 1024], BF16)
        T1 = Z.ps("T1", [128, 1024], BF16)
        T2 = Z.ps("T2", [128, 1024], BF16)
        Y = [Z.ps("Y%d" % i, [128, 512]) for i in range(2)]
        R = [Z.ps("R%d" % i, [128, 512]) for i in range(3)]
        if final:
            g_fin = Z.sb("g_fin", [128, D], F32)
            o_sb = [Z.sb("o%d" % i, [128, D], F32) for i in range(2)]

        woB = [Buf("w_out_a"), Buf("w_out_b")]
        for nb_ in range(2):
            P.dma_group("pool", ("w", 0 if nb_ == 0 else 7),
                        [(w_out[:, k, nb_ * 512:(nb_ + 1) * 512], w_out_d[k * 128:(k + 1) * 128, nb_ * 512:(nb_ + 1) * 512])
                         for k in range(16)], writes=[woB[nb_]])
        P.dma_group("pool", ("w", 1), [(wg[:, k, :], ple_w_gate[li, k * 128:(k + 1) * 128, :]) for k in range(8)],
                    writes=[wg])
        P.dma_group("pool", ("w", 2), [(wp[:, k, :], ple_w_proj[li, k * 128:(k + 1) * 128, :]) for k in range(2)],
                    writes=[wp])
        P.dma("sp", ("w", 3), g_ple[:], ple_norm[li].partition_broadcast(128), writes=[g_ple])
        if final:
            P.dma("sp", ("w", 6), g_fin[:], final_norm.partition_broadcast(128), writes=[g_fin])

        r_rot = [0]

        def nextR():
            r_rot[0] = (r_rot[0] + 1) % 3
            return R[r_rot[0]]

        def s0(c):
            s = c % 2
            P.dma("sp", ("onl", s), on_sb[s][:], on_d[c * 128:(c + 1) * 128, :], reads=[on_bufs[c]], writes=[on_sb[s]])

        def s1(c):
            s = c % 2
            oc = on_sb[s]
            P.dma("sp", ("hx", s), hx[s][:], h_src[c * 128:(c + 1) * 128, :], reads=[h_bufs[c]], writes=[hx[s]])
            for k in range(16):
                T = T0 if k < 8 else T1
                P.tr(T[:, (k % 8) * 128:(k % 8 + 1) * 128], oc[:, k * 128:(k + 1) * 128], ident[:], [oc, ident], [T],
                     signal=(k % 8 == 7))
            P.copy("dve", onTa[s][:].rearrange("p k n -> p (k n)"), T0[:], [T0], [onTa[s]])
            P.copy("act", onTb[s][:].rearrange("p k n -> p (k n)"), T1[:], [T1], [onTb[s]])

        def s2(c):
            s = c % 2
            hc = hx[s]
            h1c = h1[c % 3]
            m, rs = ms[c % 4], rstd[c % 4]
            P.dma("pool", ("pl", s), p_sb[s][:], p_in[li, c * 128:(c + 1) * 128, :], writes=[p_sb[s]])
            for nb in range(2):
                y = Y[nb]
                for k in range(16):
                    oT = onTa[s] if k < 8 else onTb[s]
                    P.mm(y[:], oT[:, k % 8, :], w_out[:, k, nb * 512:(nb + 1) * 512], k == 0, k == 15,
                         [oT, woB[nb]], [y])
                P.tt("dve", h1c[:, nb * 512:(nb + 1) * 512], y[:], hc[:, nb * 512:(nb + 1) * 512], ALU.add,
                     [y, hc], [h1c])
            P.act(junk[:], h1c[:], AF.Square, [h1c], [junk, m], accum_out=m[:, 0:1])
            rstd_from_ms(P, m[:, 0:1], rs[:, 0:1], D, [m], [rs])
            P.stt(hn2[s][:], h1c[:], rs[:, 0:1], g_ple[:], ALU.mult, ALU.mult, [h1c, rs, g_ple], [hn2[s]])

        def s3(c):
            s = c % 2
            for k in range(8):
                P.tr(T2[:, k * 128:(k + 1) * 128], hn2[s][:, k * 128:(k + 1) * 128], ident[:], [hn2[s], ident], [T2],
                     signal=(k == 7))
            P.copy("dve", hn2T[s][:].rearrange("p k n -> p (k n)"), T2[:], [T2], [hn2T[s]])
            pc = p_sb[s]
            for k in range(2):
                P.tr(T0[:, k * 128:(k + 1) * 128], pc[:, k * 128:(k + 1) * 128], ident[:], [pc, ident], [T0],
                     signal=(k == 1))
            P.copy("act", pT[s][:].rearrange("p k n -> p (k n)"), T0[:, 0:256], [T0], [pT[s]])

        def s4(c):
            s = c % 2
            h1c = h1[c % 3]
            hout = h2[s]
            m, rs = ms[c % 4], rstd[c % 4]
            for nb in range(2):
                gb = nextR()
                for k in range(8):
                    P.mm(gb[:], hn2T[s][:, k, :], wg[:, k, nb * 512:(nb + 1) * 512], k == 0, k == 7,
                         [hn2T[s], wg], [gb])
                P.act(sg[:, nb * 512:(nb + 1) * 512], gb[:], AF.Sigmoid, [gb], [sg])
                pb = nextR()
                for k in range(2):
                    P.mm(pb[:], pT[s][:, k, :], wp[:, k, nb * 512:(nb + 1) * 512], k == 0, k == 1, [pT[s], wp], [pb])
                P.tt("dve", sg[:, nb * 512:(nb + 1) * 512], pb[:], sg[:, nb * 512:(nb + 1) * 512], ALU.mult,
                     [pb, sg], [sg])
                P.tt("dve", hout[:, nb * 512:(nb + 1) * 512], sg[:, nb * 512:(nb + 1) * 512],
                     h1c[:, nb * 512:(nb + 1) * 512], ALU.add, [sg, h1c], [hout])
            if final:
                P.act(junk[:], hout[:], AF.Square, [hout], [junk, m], accum_out=m[:, 1:2])
                rstd_from_ms(P, m[:, 1:2], rs[:, 1:2], D, [m], [rs])
                P.stt(o_sb[s][:], hout[:], rs[:, 1:2], g_fin[:], ALU.mult, ALU.mult, [hout, rs, g_fin], [o_sb[s]])
                P.dma("sp", ("os", s), out[c * 128:(c + 1) * 128, :], o_sb[s][:], reads=[o_sb[s]], writes=[out_bufs[c]])
            else:
                P.dma("sp", ("hs", s), hbuf[c * 128:(c + 1) * 128, :], hout[:], reads=[hout], writes=[h_bufs[c]])

        run_pipeline([s0, s1, s2, s3, s4], NCH)
        Z.end()

    def phase_kv():
        Z = Phase(nc, P, "kv")
        wkv = Z.sb("wkv", [128, 8, 4096], BF16)
        g_kv = Z.sb("g_kv", [128, D], F32)
        hx = [Z.sb("hx%d" % i, [128, D], F32) for i in range(2)]
        junk = Z.sb("junk", [128, D], BF16)
        kvn = [Z.sb("kvn%d" % i, [128, D], BF16) for i in range(2)]
        kvnT = [Z.sb("kvnT%d" % i, [128, 8, 128], BF16) for i in range(2)]
        kT_sb = [Z.sb("kT%d" % i, [128, 16, 128], BF16) for i in range(2)]
        v_o = [Z.sb("vo%d" % i, [128, MIX], BF16) for i in range(2)]
        ms = [Z.sb("ms%d" % i, [128, 2], F32) for i in range(4)]
        rstd = [Z.sb("rstd%d" % i, [128, 2], F32) for i in range(4)]
        T0 = Z.ps("T0", [128, 1024], BF16)
        Bk = [Z.ps("B%d" % i, [128, 512]) for i in range(6)]
        wkvK, wkvV = Buf("wkvK"), Buf("wkvV")
        P.dma_group("pool", ("w", 4), [(wkv[:, k, 0:2048], w_kv[k * 128:(k + 1) * 128, 0:2048]) for k in range(8)],
                    writes=[wkvK], max_dma_last_dim=4096)
        P.dma_group("pool", ("w", 6), [(wkv[:, k, 2048:4096], w_kv[k * 128:(k + 1) * 128, 2048:4096])
                                       for k in range(8)], writes=[wkvV], max_dma_last_dim=4096)
        P.dma("sp", ("w", 5), g_kv[:], kv_norm.partition_broadcast(128), writes=[g_kv])
        b_rot = [0]

        def nextB():
            b_rot[0] = (b_rot[0] + 1) % 6
            return Bk[b_rot[0]]

        def s0(c):
            s = c % 2
            P.dma("sp", ("hx", s), hx[s][:], hbuf[c * 128:(c + 1) * 128, :], reads=[h_bufs[c]], writes=[hx[s]])

        def s1(c):
            s = c % 2
            m, rs = ms[c % 4], rstd[c % 4]
            P.act(junk[:], hx[s][:], AF.Square, [hx[s]], [junk, m], accum_out=m[:, 0:1])
            rstd_from_ms(P, m[:, 0:1], rs[:, 0:1], D, [m], [rs])
            P.stt(kvn[s][:], hx[s][:], rs[:, 0:1], g_kv[:], ALU.mult, ALU.mult, [hx[s], rs, g_kv], [kvn[s]])

        def s2(c):
            s = c % 2
            for k in range(8):
                P.tr(T0[:, k * 128:(k + 1) * 128], kvn[s][:, k * 128:(k + 1) * 128], ident[:], [kvn[s], ident], [T0],
                     signal=(k == 7))
            P.copy("dve", kvnT[s][:].rearrange("p k n -> p (k n)"), T0[:], [T0], [kvnT[s]])

        def s3(c):
            s = c % 2
            kts = kT_sb[s]
            for cg in range(4):
                kb = nextB()
                for ci in range(4):
                    ct = cg * 4 + ci
                    for k in range(8):
                        P.mm(kb[:, ci * 128:(ci + 1) * 128], wkv[:, k, ct * 128:(ct + 1) * 128], kvnT[s][:, k, :],
                             k == 0, k == 7, [wkvK, kvnT[s]], [kb], signal=(k == 7 and ci == 3))
                P.copy("act" if cg % 2 else "dve", kts[:, cg * 4:(cg + 1) * 4, :].rearrange("p k n -> p (k n)"),
                       kb[:], [kb], [kts])
            P.dma("pool", ("ks", s), kT_d[:, :, c * 128:(c + 1) * 128].rearrange("t d n -> d t n"), kts[:],
                  reads=[kts], writes=[kv_buf])
            vo = v_o[s]
            for nb in range(4):
                vb = nextB()
                for k in range(8):
                    P.mm(vb[:], kvnT[s][:, k, :], wkv[:, k, 2048 + nb * 512:2048 + (nb + 1) * 512], k == 0, k == 7,
                         [kvnT[s], wkvV], [vb])
                P.copy("act" if nb % 2 else "dve", vo[:, nb * 512:(nb + 1) * 512], vb[:], [vb], [vo])
            P.dma("pool", ("vs", s), v_d[c * 128:(c + 1) * 128, :], vo[:], reads=[vo], writes=[kv_buf])

        run_pipeline([s0, s1, s2, s3], NCH)
        Z.end()

    def phase_diff(li):
        j = li - 2
        lam_init = 0.8 - 0.6 * math.exp(-0.3 * li)
        Z = Phase(nc, P, "d%d" % li)
        hnT = Z.sb("hnT", [128, 8, S], BF16)
        g_mix = Z.sb("g_mix", [128, D], F32)
        gd = Z.sb("gd", [128, 256], F32)
        hx = [Z.sb("hx%d" % i, [128, D], F32) for i in range(2)]
        ms = Z.sb("ms", [128, 4], F32)
        rstd = Z.sb("rstd", [128, 4], F32)
        lam4 = Z.sb("lam4", [128, 4], F32)
        prods = Z.sb("prods", [128, 2], F32)
        lam = Z.sb("lam", [128, 4], F32)
        wq = [Z.sb("wq%d" % i, [128, 8, 256], BF16) for i in range(2)]
        wgt = [Z.sb("wgt%d" % i, [128, 8, 256], BF16) for i in range(2)]
        kTh = [Z.sb("kTh%d" % i, [128, 2, S], BF16) for i in range(2)]
        vh = [Z.sb("vh%d" % i, [128, NCH, 258], BF16) for i in range(2)]
        qTh = Z.sb("qTh", [128, 2, S], BF16)
        PT = [Z.sb("PT%d" % i, [128, 512], BF16) for i in range(4)]
        o_sb = [Z.sb("o%d" % i, [128, 256], F32) for i in range(2)]
        oraw = [Z.sb("or%d" % q, [128, 2, 258], F32) for q in range(2)]
        sil_h = Z.sb("sil_h", [128, NCH, 256], BF16)
        on_o = [Z.sb("ono%d" % i, [128, 256], BF16) for i in range(2)]
        rr = Z.sb("rr", [128, 4], F32)
        Sb = [Z.ps("S%d" % i, [128, 512]) for i in range(4)]
        T0 = TB(Sb[3].t.bitcast(BF16), "T0")
        T0.b = Sb[3].b
        Oa = [[Z.ps("O%d%d" % (q, t), [128, 512]) for t in range(2)] for q in range(2)]
        Gb = Sb[2]

        P.dma("sp", ("w", 0), g_mix[:], norm_mix[li].partition_broadcast(128), writes=[g_mix])
        P.dma("sp", ("w", 1), gd[:], diff_norm[j].partition_broadcast(128), writes=[gd])
        P.ts("dve", gd[:], gd[:], 1.0 - lam_init, None, ALU.mult, None, [gd], [gd])
        P.dma("sp", ("w", 2), lam4[:], diff_lambda[j].rearrange("f d -> d f"), writes=[lam4],
              allow_slow_non_contiguous=True)
        P.tt("dve", prods[:, 0:1], lam4[:, 0:1], lam4[:, 1:2], ALU.mult, [lam4], [prods])
        P.tt("dve", prods[:, 1:2], lam4[:, 2:3], lam4[:, 3:4], ALU.mult, [lam4], [prods])
        P.mm(Gb[:, 0:2], ones32[:], prods[:], True, True, [ones32, prods], [Gb])
        P.act(lam[:, 0:2], Gb[:, 0:2], AF.Exp, [Gb], [lam])
        P.tt("dve", lam[:, 2:3], lam[:, 1:2], lam[:, 0:1], ALU.subtract, [lam], [lam])
        P.ts("dve", lam[:, 2:3], lam[:, 2:3], -lam_init, None, ALU.add, None, [lam], [lam])

        for i in range(2):
            P.memset("pool", vh[i][:, :, 256:258], 1.0, [vh[i]])

        def head_loads(hd):
            s = hd % 2
            P.dma_group("pool", ("wq", s), [(wq[s][:, k, :],
                                             diff_w_in[j, k * 128:(k + 1) * 128, hd * 256:(hd + 1) * 256])
                                            for k in range(8)], writes=[wq[s]])
            P.dma_group("pool", ("wg", s), [(wgt[s][:, k, :],
                                             diff_w_in[j, k * 128:(k + 1) * 128, 2048 + hd * 256:2048 + (hd + 1) * 256])
                                            for k in range(8)], writes=[wgt[s]])
            P.dma("sp", ("kh", s), kTh[s][:], kT_d[2 * hd:2 * hd + 2].rearrange("t d n -> d t n"),
                  reads=[kv_buf], writes=[kTh[s]])
            P.dma_group("sp", ("vh", s), [(vh[s][:, g * 8:(g + 1) * 8, 0:256],
                                           v_d[g * 1024:(g + 1) * 1024, hd * 256:(hd + 1) * 256].rearrange(
                                               "(n p) c -> p n c", p=128)) for g in range(4)],
                        reads=[kv_buf], writes=[vh[s]])

        head_loads(0)
        msP = [Z.sb("msP%d" % i, [128, 2], F32) for i in range(2)]
        rsP = [Z.sb("rsP%d" % i, [128, 2], F32) for i in range(2)]
        hnS = [qTh[:, i, 0:1024] for i in range(2)]
        hnB = [Buf("hnS0"), Buf("hnS1")]
        jnk = qTh[:, 0, 2048:3072]
        jnkB = Buf("jnk")

        def p0(c):
            P.dma("sp", ("hx", c % 2), hx[c % 2][:], hbuf[c * 128:(c + 1) * 128, :],
                  reads=[h_bufs[c]], writes=[hx[c % 2]])

        def p1(c):
            hc, m, rs = hx[c % 2], msP[c % 2], rsP[c % 2]
            P.act(jnk, hc[:], AF.Square, [hc], [jnkB, m], accum_out=m[:, 0:1])
            rstd_from_ms(P, m[:, 0:1], rs[:, 0:1], D, [m], [rs])
            P.stt(hnS[c % 2], hc[:], rs[:, 0:1], g_mix[:], ALU.mult, ALU.mult, [hc, rs, g_mix], [hnB[c % 2]])

        def p2(c):
            src = hnS[c % 2]
            for k in range(8):
                P.tr(T0[:, k * 128:(k + 1) * 128], src[:, k * 128:(k + 1) * 128], ident[:], [hnB[c % 2], ident], [T0],
                     signal=(k == 7))
            P.copy("dve", hnT[:, :, c * 128:(c + 1) * 128], T0[:].rearrange("p (k n) -> p k n", k=8), [T0], [hnT])

        run_pipeline([p0, p1, p2], NCH)

        scale = 128 ** -0.5
        LA = 3
        silB = [Buf("sil%d" % i) for i in range(16)]
        for hd in range(8):
            s = hd % 2
            if hd + 1 < 8:
                head_loads(hd + 1)
            for t in range(2):
                for g in range(8):
                    sb_ = Sb[g % 3]
                    for k in range(8):
                        P.mm(sb_[:], wq[s][:, k, t * 128:(t + 1) * 128], hnT[:, k, g * 512:(g + 1) * 512],
                             k == 0, k == 7, [wq[s], hnT], [sb_])
                    ev = "act" if (g % 2 or (t == 0 and g < 4)) else "dve"
                    P.copy(ev, qTh[:, t, g * 512:(g + 1) * 512], sb_[:], [sb_], [qTh])
            for pr in range(16):
                sb_ = Sb[pr % 3]
                for q in range(2):
                    qt_i = 2 * pr + q
                    for k in range(8):
                        P.mm(sb_[:, q * 256:(q + 1) * 256], hnT[:, k, qt_i * 128:(qt_i + 1) * 128], wgt[s][:, k, :],
                             k == 0, k == 7, [hnT, wgt[s]], [sb_], signal=(k == 7 and q == 1))
                P.act(sil_h[:, 2 * pr:2 * pr + 2, :].rearrange("p a c -> p (a c)"), sb_[:], AF.Silu, [sb_], [silB[pr]])
                for q in range(2):
                    P.tt("pool", sil_h[:, 2 * pr + q, :], sil_h[:, 2 * pr + q, :], gd[:], ALU.mult,
                         [silB[pr], gd], [silB[pr]])

            items = [(r, kt_) for r in range(16) for kt_ in range(2 * r + 2)]

            def emit_ST(i):
                r, kt_ = items[i]
                sb_ = Sb[i % 4]
                if kt_ == 2 * r + 1:
                    for t in range(2):
                        reg = sb_[:, t * 256 + 128:(t + 1) * 256]
                        P.mm(reg, kTh[s][:, t, kt_ * 128:(kt_ + 1) * 128], qTh[:, t, r * 256 + 128:(r + 1) * 256],
                             True, False, [kTh[s], qTh], [sb_], signal=False)
                        P.mm(reg, ident[:], maskb[:], False, True, [ident, maskb], [sb_], signal=(t == 1))
                else:
                    for t in range(2):
                        dg = (kt_ == 2 * r)
                        P.mm(sb_[:, t * 256:(t + 1) * 256], kTh[s][:, t, kt_ * 128:(kt_ + 1) * 128],
                             qTh[:, t, r * 256:(r + 1) * 256], True, not dg, [kTh[s], qTh], [sb_],
                             signal=(t == 1 and not dg))
                        if dg:
                            P.mm(sb_[:, t * 256:t * 256 + 128], ident[:], maskb[:], False, True, [ident, maskb], [sb_],
                                 signal=(t == 1))

            def emit_exp(i):
                r, kt_ = items[i]
                sb_ = Sb[i % 4]
                pt = PT[i % 4]
                if kt_ != 2 * r + 1:
                    P.act(pt[:], sb_[:], AF.Exp, [sb_], [pt], scale=scale)
                else:
                    v3 = sb_[:].rearrange("p (t q) -> p t q", t=2)[:, :, 128:256]
                    p3 = pt[:].rearrange("p (t q) -> p t q", t=2)[:, :, 128:256]
                    P.act(p3, v3, AF.Exp, [sb_], [pt], scale=scale)

            def emit_PV(i):
                r, kt_ = items[i]
                pt = PT[i % 4]
                qtiles = (0, 1) if kt_ != 2 * r + 1 else (1,)
                for q in qtiles:
                    last = (kt_ == 2 * r + q)
                    for t in range(2):
                        P.mm(Oa[q][t][:, 0:257], pt[:, t * 256 + q * 128:t * 256 + (q + 1) * 128],
                             vh[s][:, kt_, 0:257], kt_ == 0, last, [pt, vh[s]], [Oa[q][t]],
                             signal=(last or t == 1))

            def fin_A1(r, q):
                O1, O2 = Oa[q][0], Oa[q][1]
                rw = oraw[q]
                P.copy("dve", rw[:, 0, 0:257], O1[:, 0:257], [O1], [rw])
                P.copy("dve", rw[:, 1, 0:257], O2[:, 0:257], [O2], [rw])

            def fin_A2(r, q):
                o_ = o_sb[q]
                rw = oraw[q]
                rq = rr[:, 2 * q:2 * q + 2]
                P.recip(rq, rw[:, :, 256], [rw], [rr])
                P.ts("dve", o_[:], rw[:, 1, 0:256], rr[:, 2 * q + 1:2 * q + 2], lam[:, 2:3], ALU.mult, ALU.mult,
                     [rw, rr, lam], [o_])
                P.stt(o_[:], rw[:, 0, 0:256], rr[:, 2 * q:2 * q + 1], o_[:], ALU.mult, ALU.add, [rw, rr, o_], [o_])
                P.op("dve", lambda h, o_=o_, q=q: h.scalar_tensor_tensor(
                    out=on_o[q][:], in0=o_[:], scalar=1.0, in1=o_[:], op0=ALU.mult, op1=ALU.mult,
                    accum_out=ms[:, 1 + q:2 + q]), [o_], [on_o[q], ms])

            def fin_B(r, q):
                qt_i = 2 * r + q
                o_ = o_sb[q]
                rstd_from_ms(P, ms[:, 1 + q:2 + q], rstd[:, 1 + q:2 + q], 256, [ms], [rstd])
                oo = on_o[q]
                P.stt(oo[:], o_[:], rstd[:, 1 + q:2 + q], sil_h[:, qt_i, :], ALU.mult, ALU.mult,
                      [o_, rstd, silB[qt_i // 2]], [oo])
                P.dma("pool", ("ono", q), on_d[qt_i * 128:(qt_i + 1) * 128, hd * 256:(hd + 1) * 256], oo[:],
                      reads=[oo], writes=[on_bufs[qt_i]])

            deferred = []
            n_it = len(items)
            for i in range(min(LA, n_it)):
                emit_ST(i)
            for i in range(n_it):
                r, kt_ = items[i]
                if i + LA < n_it:
                    emit_ST(i + LA)
                emit_exp(i)
                emit_PV(i)
                for (due, fn) in [d for d in deferred if d[0] <= i]:
                    fn()
                deferred = [d for d in deferred if d[0] > i]
                if kt_ == 2 * r:
                    fin_A1(r, 0)
                if kt_ == 2 * r + 1:
                    fin_A1(r, 1)
                    fin_A2(r, 0)
                    fin_A2(r, 1)
                    deferred.append((i + 2, lambda r=r: fin_B(r, 0)))
                    deferred.append((i + 2, lambda r=r: fin_B(r, 1)))
            for (due, fn) in deferred:
                fn()
        Z.end()

    P.barrier()
    for ph in phases:
        kind, li = ph[0], int(ph[1])
        if kind == "g":
            phase_gla(li, x if li == 0 else hbuf)
        elif kind == "b":
            wo = gla_w_out[li] if li < 2 else diff_w_out[li - 2]
            phase_b(li, x if li == 0 else hbuf, wo, final=(li == 3))
            if li == 1:
                phase_kv()
        elif kind == "d":
            phase_diff(li)
    G.es.close()
    P.close()
    return nc, P


_CACHE = {}


def _in_maps(inputs):
    maps = []
    shared = {k: np.ascontiguousarray(v, dtype=np.float32) for k, v in inputs.items() if k not in ("x", "p")}
    for b in range(N_CORES):
        m = dict(shared)
        m["x"] = np.ascontiguousarray(inputs["x"][b], dtype=np.float32)
        m["p"] = np.ascontiguousarray(inputs["p"][:, b], dtype=np.float32)
        maps.append(m)
    return maps


def kernel(**inputs):
    if "nc" not in _CACHE:
        _CACHE["nc"] = build_program()[0]
    nc = _CACHE["nc"]
    res = run_bass_kernel_spmd(nc, _in_maps(inputs), core_ids=list(range(N_CORES)))
    return np.stack([np.asarray(r["out"], dtype=np.float32) for r in res.results], axis=0)
```

```python
import math
from contextlib import ExitStack

import numpy as np
import concourse.bass as bass
import concourse.mybir as mybir
from concourse.bass_utils import run_bass_kernel_spmd

F32 = mybir.dt.float32
BF16 = mybir.dt.bfloat16
AF = mybir.ActivationFunctionType
ALU = mybir.AluOpType

S = 4096
D = 1024
NCH = S // 128
MIX = 2048
PLE = 256
EPS = 1e-6
GLA_IN = 6160
N_CORES = 8

SAME_ENGINE_SYNC = True


class Buf:
    __slots__ = ("name", "last_write", "reads", "psum")

    def __init__(self, name=""):
        self.name = name
        self.last_write = None
        self.reads = []
        self.psum = False


class TB:
    def __init__(self, t, name):
        self.t = t
        self.b = Buf(name)

    def __getitem__(self, k):
        return self.t[k]


class Eng:
    def __init__(self, name, sem):
        self.name = name
        self.sem = sem
        self.count = 0
        self.ops = []
        self.waited = {}
        self.pending = False


class Prog:
    def __init__(self, nc):
        self.nc = nc
        self.es = ExitStack()
        self.engs = {}
        for n in ("pe", "act", "dve", "pool", "sp"):
            sem = self.es.enter_context(nc.semaphore("sem_" + n))
            self.engs[n] = Eng(n, sem)
        self.dma_sems = {}
        self.n_ops = 0

    def dma_sem(self, key):
        if key not in self.dma_sems:
            sem = self.es.enter_context(self.nc.semaphore("dsem_%d" % len(self.dma_sems)))
            self.dma_sems[key] = [sem, 0]
        return self.dma_sems[key]

    def _deps(self, reads, writes):
        deps = []
        for b in reads:
            if b.last_write is not None:
                deps.append(b.last_write)
            if b.psum:
                deps.extend(b.reads)
        for b in writes:
            if b.last_write is not None:
                deps.append(b.last_write)
            deps.extend(b.reads)
        return deps

    def _emit_waits(self, e, deps):
        need = {}
        for (sem, val) in deps:
            if sem is e.sem:
                if e.name == "pe" or not SAME_ENGINE_SYNC:
                    continue
                if val > e.count:
                    continue
            k = id(sem)
            if e.waited.get(k, 0) >= val:
                continue
            if k not in need or need[k][1] < val:
                need[k] = (sem, val)
        for k, (sem, val) in need.items():
            e.waited[k] = val
            e.ops.append(("wait", sem, val))

    def _check_pending(self, e):
        for o in self.engs.values():
            if o is not e and o.pending:
                raise RuntimeError("op on %s while %s has an unsignalled group open" % (e.name, o.name))

    def op(self, eng, fn, reads=(), writes=(), signal=True):
        e = self.engs[eng]
        self._check_pending(e)
        reads = [r.b if isinstance(r, TB) else r for r in reads]
        writes = [w.b if isinstance(w, TB) else w for w in writes]
        self._emit_waits(e, self._deps(reads, writes))
        self.n_ops += 1
        if signal:
            e.count += 1
            tok = (e.sem, e.count)
            e.ops.append(("op", fn, e.sem, 1))
            e.pending = False
        else:
            tok = (e.sem, e.count + 1)
            e.ops.append(("op", fn, None, 0))
            e.pending = True
        for b in reads:
            b.reads.append(tok)
        for b in writes:
            b.last_write = tok
            b.reads = []
        return tok

    def dma(self, queue, key, out, in_, reads=(), writes=(), **kw):
        e = self.engs[queue]
        self._check_pending(e)
        assert not e.pending
        reads = [r.b if isinstance(r, TB) else r for r in reads]
        writes = [w.b if isinstance(w, TB) else w for w in writes]
        self._emit_waits(e, self._deps(reads, writes))
        ds = self.dma_sem(key)
        ds[1] += 16
        tok = (ds[0], ds[1])
        e.ops.append(("op", lambda h, o=out, i=in_, kw=kw: h.dma_start(out=o, in_=i, **kw), ds[0], 16))
        self.n_ops += 1
        for b in reads:
            b.reads.append(tok)
        for b in writes:
            b.last_write = tok
            b.reads = []
        return tok

    def dma_group(self, queue, key, pairs, reads=(), writes=(), **kw):
        e = self.engs[queue]
        self._check_pending(e)
        reads = [r.b if isinstance(r, TB) else r for r in reads]
        writes = [w.b if isinstance(w, TB) else w for w in writes]
        self._emit_waits(e, self._deps(reads, writes))
        ds = self.dma_sem(key)
        for (out, in_) in pairs:
            ds[1] += 16
            e.ops.append(("op", lambda h, o=out, i=in_, kw=kw: h.dma_start(out=o, in_=i, **kw), ds[0], 16))
            self.n_ops += 1
        tok = (ds[0], ds[1])
        for b in reads:
            b.reads.append(tok)
        for b in writes:
            b.last_write = tok
            b.reads = []
        return tok

    def barrier(self):
        for e in self.engs.values():
            assert not e.pending
        toks = [(e.sem, e.count) for e in self.engs.values() if e.count > 0]
        toks += [(s, v) for (s, v) in self.dma_sems.values() if v > 0]
        for e in self.engs.values():
            self._emit_waits(e, [t for t in toks if t[0] is not e.sem])

    def emit_block(self):
        nc = self.nc
        for e in self.engs.values():
            assert not e.pending, e.name

        def replay(h, e):
            for rec in e.ops:
                if rec[0] == "wait":
                    h.wait_ge(rec[1], rec[2])
                else:
                    ins = rec[1](h)
                    if rec[2] is not None:
                        ins.then_inc(rec[2], rec[3])
            e.ops = []

        with nc.Block() as block:
            @block.tensor
            def _(h):
                replay(h, self.engs["pe"])

            @block.scalar
            def _(h):
                replay(h, self.engs["act"])

            @block.vector
            def _(h):
                replay(h, self.engs["dve"])

            @block.gpsimd
            def _(h):
                replay(h, self.engs["pool"])

            @block.sync
            def _(h):
                replay(h, self.engs["sp"])

    def close(self):
        self.es.close()

    def mm(self, out, lhsT, rhs, start, stop, reads, writes, signal=None):
        if signal is None:
            signal = stop
        return self.op("pe", lambda h: h.matmul(out, lhsT=lhsT, rhs=rhs, start=start, stop=stop),
                       reads, writes, signal=signal)

    def tr(self, out, in_, ident, reads, writes, signal=True):
        return self.op("pe", lambda h: h.transpose(out, in_, ident), reads, writes, signal=signal)

    def act(self, out, in_, func, reads, writes, eng="act", **kw):
        return self.op(eng, lambda h: h.activation(out=out, in_=in_, func=func, **kw), reads, writes)

    def tt(self, eng, out, in0, in1, op, reads, writes):
        return self.op(eng, lambda h: h.tensor_tensor(out=out, in0=in0, in1=in1, op=op), reads, writes)

    def ts(self, eng, out, in0, s1, s2, op0, op1, reads, writes):
        if s2 is None:
            return self.op(eng, lambda h: h.tensor_scalar(out=out, in0=in0, scalar1=s1, scalar2=None, op0=op0),
                           reads, writes)
        return self.op(eng, lambda h: h.tensor_scalar(out=out, in0=in0, scalar1=s1, scalar2=s2, op0=op0, op1=op1),
                       reads, writes)

    def stt(self, out, in0, scalar, in1, op0, op1, reads, writes):
        return self.op("dve", lambda h: h.scalar_tensor_tensor(out=out, in0=in0, scalar=scalar, in1=in1,
                                                                 op0=op0, op1=op1), reads, writes)

    def copy(self, eng, out, in_, reads, writes):
        if eng == "act":
            return self.op("act", lambda h: h.activation(out=out, in_=in_, func=AF.Copy), reads, writes)
        return self.op(eng, lambda h: h.tensor_copy(out=out, in_=in_), reads, writes)

    def memset(self, eng, ap, val, writes):
        return self.op(eng, lambda h: h.memset(ap, val), [], writes)

    def recip(self, out, in_, reads, writes):
        return self.op("dve", lambda h: h.reciprocal(out=out, in_=in_), reads, writes)


class Phase:
    def __init__(self, nc, P, name):
        self.nc, self.P, self.name = nc, P, name
        self.es = ExitStack()
        self.k = 0

    def sb(self, name, shape, dtype):
        self.k += 1
        t = self.es.enter_context(self.nc.sbuf_tensor("%s_%s" % (self.name, name), list(shape), dtype))
        return TB(t, name)

    def ps(self, name, shape, dtype=F32):
        t = self.es.enter_context(self.nc.psum_tensor("%s_%s" % (self.name, name), list(shape), dtype))
        tb = TB(t, name)
        tb.b.psum = True
        return tb

    def end(self):
        self.P.barrier()
        self.P.emit_block()
        self.es.close()


def rstd_from_ms(P, ms_ap, out_ap, n, reads, writes, eng="pool"):
    if eng == "act":
        P.act(out_ap, ms_ap, AF.Ln, reads, writes, scale=1.0 / n, bias=EPS)
        P.act(out_ap, out_ap, AF.Exp, writes, writes, scale=-0.5)
        return
    nh = P.nhalf
    P.ts("pool", out_ap, ms_ap, 1.0 / n, EPS, ALU.mult, ALU.add, reads, writes)
    P.tt("pool", out_ap, out_ap, nh[:, 0:1], ALU.pow, list(writes) + [nh], writes)


def build_program(debug=False, phases=("g0", "b0", "g1", "b1", "d2", "b2", "d3", "b3")):
    nc = bass.Bass("TRN2", target_bir_lowering=False)

    def din(name, shape):
        return nc.dram_tensor(name, list(shape), F32, kind="ExternalInput").ap()

    x = din("x", [S, D])
    p_in = din("p", [4, S, PLE])
    norm_mix = din("norm_mix", [4, D])
    gla_w_in = din("gla_w_in", [2, D, GLA_IN])
    gla_w_gk2 = din("gla_w_gk2", [2, 16, 1024])
    gla_b_gk = din("gla_b_gk", [2, 1024])
    gla_norm = din("gla_norm", [2, 512])
    gla_w_out = din("gla_w_out", [2, MIX, D])
    kv_norm = din("kv_norm", [D])
    w_kv = din("w_kv", [D, 4096])
    diff_w_in = din("diff_w_in", [2, D, 4096])
    diff_lambda = din("diff_lambda", [2, 4, 128])
    diff_norm = din("diff_norm", [2, 256])
    diff_w_out = din("diff_w_out", [2, MIX, D])
    ple_norm = din("ple_norm", [4, D])
    ple_w_gate = din("ple_w_gate", [4, D, D])
    ple_w_proj = din("ple_w_proj", [4, PLE, D])
    final_norm = din("final_norm", [D])

    out = nc.dram_tensor("out", [S, D], F32, kind="ExternalOutput").ap()
    skind = "ExternalOutput" if debug else "Internal"
    hbuf = nc.dram_tensor("hbuf", [S, D], F32, kind=skind).ap()
    on_d = nc.dram_tensor("on_d", [S, MIX], BF16, kind=skind).ap()
    kT_d = nc.dram_tensor("kT_d", [16, 128, S], BF16, kind=skind).ap()
    v_d = nc.dram_tensor("v_d", [S, MIX], BF16, kind=skind).ap()

    P = Prog(nc)
    h_bufs = [Buf("h%d" % c) for c in range(NCH)]
    on_bufs = [Buf("on%d" % c) for c in range(NCH)]
    out_bufs = [Buf("out%d" % c) for c in range(NCH)]
    kv_buf = Buf("kvd")

    G = Phase(nc, P, "c")
    ident = G.sb("ident", [128, 128], BF16)
    mask2 = G.sb("mask2", [128, 4, 128], BF16)
    U32 = G.sb("U32", [128, 128], F32)
    L32 = G.sb("L32", [128, 128], F32)
    ones32 = G.sb("ones32", [128, 128], F32)
    P.memset("pool", ident[:], 0.0, [ident])
    P.op("pool", lambda h: h.affine_select(out=ident[:], in_=ident[:], pattern=[[-1, 128]],
                                           compare_op=ALU.not_equal, fill=1.0, base=0, channel_multiplier=1),
         [ident], [ident])
    P.memset("pool", mask2[:], 1.0, [mask2])
    for r in range(4):
        P.op("pool", lambda h, r=r: h.affine_select(out=mask2[:, r, :], in_=mask2[:, r, :], pattern=[[1, 128]],
                                                    compare_op=ALU.is_ge, fill=0.0, base=0, channel_multiplier=-1),
             [mask2], [mask2])
    P.memset("pool", U32[:], 1.0, [U32])
    P.op("pool", lambda h: h.affine_select(out=U32[:], in_=U32[:], pattern=[[1, 128]],
                                           compare_op=ALU.is_ge, fill=0.0, base=0, channel_multiplier=-1),
         [U32], [U32])
    P.memset("pool", L32[:], 1.0, [L32])
    P.op("pool", lambda h: h.affine_select(out=L32[:], in_=L32[:], pattern=[[-1, 128]],
                                           compare_op=ALU.is_gt, fill=0.0, base=0, channel_multiplier=1),
         [L32], [L32])
    P.memset("pool", ones32[:], 1.0, [ones32])
    L16 = G.sb("L16", [128, 128], BF16)
    P.memset("pool", L16[:], 1.0, [L16])
    P.op("pool", lambda h: h.affine_select(out=L16[:], in_=L16[:], pattern=[[-1, 128]],
                                           compare_op=ALU.is_gt, fill=0.0, base=0, channel_multiplier=1),
         [L16], [L16])
    ones16 = G.sb("ones16", [128, 2], BF16)
    P.memset("pool", ones16[:], 1.0, [ones16])
    maskb = G.sb("maskb", [128, 128], BF16)
    P.memset("pool", maskb[:], -30000.0, [maskb])
    P.op("pool", lambda h: h.affine_select(out=maskb[:], in_=maskb[:], pattern=[[-1, 128]],
                                           compare_op=ALU.is_gt, fill=0.0, base=0, channel_multiplier=1),
         [maskb], [maskb])
    nhalf = G.sb("nhalf", [128, 2], F32)
    P.memset("pool", nhalf[:], -0.5, [nhalf])
    P.nhalf = nhalf

    def run_pipeline(stages, n):
        ns = len(stages)
        for step in range(n + ns - 1):
            for k in range(ns - 1, -1, -1):
                c = step - k
                if 0 <= c < n:
                    stages[k](c)

    def phase_gla(li, h_src):
        Z = Phase(nc, P, "g%d" % li)
        w_in = Z.sb("w_in", [128, 8, GLA_IN], BF16)
        wgk = Z.sb("wgk", [32, 1024], BF16)
        g_mix = Z.sb("g_mix", [128, D], F32)
        gn = Z.sb("gn", [128, 512], F32)
        hx = [Z.sb("hx%d" % i, [128, D], F32) for i in range(2)]
        hn = [Z.sb("hn%d" % i, [128, D], BF16) for i in range(2)]
        hnT = [Z.sb("hnT%d" % i, [128, 8, 128], BF16) for i in range(3)]
        v_sb = Z.sb("v", [128, MIX], BF16)
        gsil = Z.sb("gsil", [128, MIX], BF16)
        gkl = Z.sb("gkl", [128, 32], BF16)
        gklT = Z.sb("gklT", [32, 128], BF16)
        sp = [Z.sb("sp%d" % i, [128, 1024], F32) for i in range(1)] * 2
        sph = Z.sb("sph", [128, 1024], BF16)
        spl = Z.sb("spl", [128, 1024], BF16)
        E = [[Z.sb("E%d_%d" % (i, k), [128, 1024], F32) for k in range(3)] for i in range(2)]
        ebl = [Z.sb("ebl%d" % i, [128, 8], F32) for i in range(2)]
        qt = Z.sb("qt", [128, 1024], BF16)
        kt = Z.sb("kt", [128, 1024], BF16)
        kd = Z.sb("kd", [128, 1024], BF16)
        qtT = Z.sb("qtT", [128, 8, 128], BF16)
        ktT = Z.sb("ktT", [128, 8, 128], BF16)
        scT = Z.sb("scT", [128, 512], BF16)
        S32 = [Z.sb("S32_%d" % i, [128, 512], F32) for i in range(8)]
        S16 = [Z.sb("S16_%d" % i, [128, 512], BF16) for i in range(8)]
        on_sb = [Z.sb("on%d" % i, [128, MIX], BF16) for i in range(1)] * 2
        ms = [Z.sb("ms%d" % i, [128, 2], F32) for i in range(2)]
        rstd = [Z.sb("rstd%d" % i, [128, 2], F32) for i in range(2)]
        mso = Z.sb("mso", [128, 4], F32)
        rso = Z.sb("rso", [128, 4], F32)
        A = [Z.ps("A%d" % i, [128, 512]) for i in range(2)]
        Tb = Z.ps("T", [128, 1024], BF16)
        Gk = [Z.ps("G%d" % i, [128, 512]) for i in range(2)]
        M = Z.ps("M", [128, 512])
        O = [Z.ps("O%d" % i, [128, 512]) for i in range(2)]
        Tf = TB(Tb.t.bitcast(F32), "Tf")
        Tf.b = Tb.b
        O4 = [O[0], O[1], M, Tf]

        wgrp = [("gk", 6144, 6160, 0), ("gate", 4096, 6144, 4), ("v", 2048, 4096, 5), ("q", 0, 1024, 6),
                ("k", 1024, 2048, 7)]
        wB = {}
        for (nm, c0, c1, key) in wgrp:
            wB[nm] = Buf("w_" + nm)

        def wbuf(col0):
            if col0 >= 6144:
                return wB["gk"]
            if col0 >= 4096:
                return wB["gate"]
            if col0 >= 2048:
                return wB["v"]
            return wB["q"] if col0 < 1024 else wB["k"]

        nm, c0, c1, key = wgrp[0]
        P.dma_group("pool", ("w", key), [(w_in[:, k, c0:c1], gla_w_in[li, k * 128:(k + 1) * 128, c0:c1])
                                         for k in range(8)], writes=[wB[nm]])
        P.dma_group("pool", ("w", 1), [(wgk[0:16, :], gla_w_gk2[li]), (wgk[16:17, :], gla_b_gk[li:li + 1, :])],
                    writes=[wgk])
        P.dma("sp", ("w", 2), g_mix[:], norm_mix[li].partition_broadcast(128), writes=[g_mix])
        P.dma("sp", ("w", 3), gn[:], gla_norm[li].partition_broadcast(128), writes=[gn])
        for (nm, c0, c1, key) in wgrp[1:]:
            P.dma_group("pool", ("w", key), [(w_in[:, k, c0:c1], gla_w_in[li, k * 128:(k + 1) * 128, c0:c1])
                                             for k in range(8)], writes=[wB[nm]], max_dma_last_dim=4096)
        for i in range(8):
            P.memset("pool", S32[i][:], 0.0, [S32[i]])
            P.memset("pool", S16[i][:], 0.0, [S16[i]])
        P.memset("pool", gkl[:], 1.0, [gkl])

        a_rot = [0]
        tmp_bufs = [[Buf("tmp%d_%d" % (i, h)) for h in range(4)] for i in range(2)]

        def nextA():
            a_rot[0] ^= 1
            return A[a_rot[0]]

        def proj(hT, col0, n, dst):
            for k in range(8):
                P.mm(dst[:, 0:n], hT[:, k, :], w_in[:, k, col0:col0 + n], k == 0, k == 7, [hT, wbuf(col0)], [dst])

        def s0(c):
            P.dma("sp", ("hx", c % 2), hx[c % 2][:], h_src[c * 128:(c + 1) * 128, :],
                  reads=[h_bufs[c]], writes=[hx[c % 2]])

        def s1(c):
            hc, m, rs, hn_ = hx[c % 2], ms[c % 2], rstd[c % 2], hn[c % 2]
            P.act(hn_[:], hc[:], AF.Square, [hc], [hn_, m], accum_out=m[:, 0:1])
            rstd_from_ms(P, m[:, 0:1], rs[:, 0:1], D, [m], [rs], eng="act")
            P.stt(hn_[:], hc[:], rs[:, 0:1], g_mix[:], ALU.mult, ALU.mult, [hc, rs, g_mix], [hn_])

        def s2(c):
            hn_, hT = hn[c % 2], hnT[c % 3]
            for k in range(8):
                P.tr(Tb[:, k * 128:(k + 1) * 128], hn_[:, k * 128:(k + 1) * 128], ident[:], [hn_, ident], [Tb],
                     signal=(k == 7))
            P.copy("dve", hT[:].rearrange("p k n -> p (k n)"), Tb[:], [Tb], [hT])

        def s3a_1(c):
            hT = hnT[c % 3]
            for k in range(8):
                P.mm(M[:, 0:16], hT[:, k, :], w_in[:, k, 6144:6160], k == 0, k == 7, [hT, wB["gk"]], [M])
            P.copy("act", gkl[:, 0:16], M[:, 0:16], [M], [gkl])

        def s3a_2(c):
            P.tr(Tb[0:32, 0:128], gkl[:], ident[:], [gkl, ident], [Tb])
            P.copy("act", gklT[:], Tb[0:32, 0:128], [Tb], [gklT])

        def s3a_3(c):
            sp_ = sp[c % 2]
            for nb in range(2):
                P.mm(Gk[nb][:], gklT[0:17, :], wgk[0:17, nb * 512:(nb + 1) * 512], True, True, [gklT, wgk], [Gk[nb]])
                P.act(sp_[:, nb * 512:(nb + 1) * 512], Gk[nb][:], AF.Exp, [Gk[nb]], [sp_], scale=-1.0)
            P.act(sp_[:], sp_[:], AF.Ln, [sp_], [sp_], bias=1.0)
            P.copy("act", sph[:], sp_[:], [sp_], [sph])
            P.tt("pool", spl[:], sp_[:], sph[:], ALU.subtract, [sp_, sph], [spl])

        def s3b(c):
            sp_, E0, E1, E2, eb = sp[c % 2], E[c % 2][0], E[c % 2][1], E[c % 2][2], ebl[c % 2]
            U16 = mask2[:, 0, :]
            for nb in range(2):
                P.mm(Gk[nb][:], U16, sph[:, nb * 512:(nb + 1) * 512], True, False, [mask2, sph], [Gk[nb]], signal=False)
                P.mm(Gk[nb][:], U16, spl[:, nb * 512:(nb + 1) * 512], False, True, [mask2, spl], [Gk[nb]])
            for f in range(8):
                P.mm(M[:, 2 * f:2 * f + 2], sph[:, f * 128:(f + 1) * 128], ones16[:, 0:2], True, False,
                     [sph, ones16], [M], signal=False)
                P.mm(M[:, 2 * f:2 * f + 2], spl[:, f * 128:(f + 1) * 128], ones16[:, 0:2], False, True,
                     [spl, ones16], [M], signal=(f == 7))
            for nb in range(2):
                P.act(E0[:, nb * 512:(nb + 1) * 512], Gk[nb][:], AF.Exp, [Gk[nb]], [E0], scale=-1.0 / 16)
                P.act(E1[:, nb * 512:(nb + 1) * 512], Gk[nb][:], AF.Exp, [Gk[nb]], [E1], scale=1.0 / 16)
            P.act(eb[:], M[:, 0:16].rearrange("p (f t) -> p f t", t=2)[:, :, 0], AF.Exp, [M], [eb], scale=-1.0 / 16)
            for nb in range(2):
                P.mm(Gk[nb][:], L16[:], sph[:, nb * 512:(nb + 1) * 512], True, False, [L16, sph], [Gk[nb]], signal=False)
                P.mm(Gk[nb][:], L16[:], spl[:, nb * 512:(nb + 1) * 512], False, True, [L16, spl], [Gk[nb]])
                P.act(E2[:, nb * 512:(nb + 1) * 512], Gk[nb][:], AF.Exp, [Gk[nb]], [E2], scale=-1.0 / 16)

        def s4_q(c):
            hT, E0 = hnT[c % 3], E[c % 2][0]
            for nb in range(2):
                a = nextA()
                proj(hT, nb * 512, 512, a)
                P.stt(qt[:, nb * 512:(nb + 1) * 512], a[:], 256 ** -0.5, E0[:, nb * 512:(nb + 1) * 512],
                      ALU.mult, ALU.mult, [a, E0], [qt])

        def s4_k(c):
            hT, E1, E2 = hnT[c % 3], E[c % 2][1], E[c % 2][2]
            for nb in range(2):
                a = nextA()
                proj(hT, 1024 + nb * 512, 512, a)
                P.tt("dve", kt[:, nb * 512:(nb + 1) * 512], a[:], E1[:, nb * 512:(nb + 1) * 512], ALU.mult,
                     [a, E1], [kt])
                P.tt("dve", kd[:, nb * 512:(nb + 1) * 512], a[:], E2[:, nb * 512:(nb + 1) * 512], ALU.mult,
                     [a, E2], [kd])

        def s4_gate(c):
            hT = hnT[c % 3]
            for nb in range(4):
                a = nextA()
                proj(hT, 4096 + nb * 512, 512, a)
                P.act(gsil[:, nb * 512:(nb + 1) * 512], a[:], AF.Silu, [a], [gsil])
            for nb in range(4):
                P.tt("pool", gsil[:, nb * 512:(nb + 1) * 512], gsil[:, nb * 512:(nb + 1) * 512], gn[:], ALU.mult,
                     [gsil, gn], [gsil])

        def s4_v(c):
            hT = hnT[c % 3]
            for nb in range(4):
                a = nextA()
                proj(hT, 2048 + nb * 512, 512, a)
                P.copy("dve", v_sb[:, nb * 512:(nb + 1) * 512], a[:], [a], [v_sb])

        def s4_tr(c):
            for k in range(8):
                P.tr(Tb[:, k * 128:(k + 1) * 128], qt[:, k * 128:(k + 1) * 128], ident[:], [qt, ident], [Tb],
                     signal=(k == 7))
            P.copy("act", qtT[:].rearrange("p k n -> p (k n)"), Tb[:], [Tb], [qtT])
            for k in range(8):
                P.tr(Tb[:, k * 128:(k + 1) * 128], kt[:, k * 128:(k + 1) * 128], ident[:], [kt, ident], [Tb],
                     signal=(k == 7))
            P.copy("dve", ktT[:].rearrange("p k n -> p (k n)"), Tb[:], [Tb], [ktT])

        def s4_rec(c):
            eb = ebl[c % 2]
            for hh in range(4):
                for dt in range(2):
                    P.mm(M[:, hh * 128:(hh + 1) * 128], ktT[:, 2 * hh + dt, :], qtT[:, 2 * hh + dt, :],
                         dt == 0, dt == 1, [ktT, qtT], [M], signal=(hh == 3 and dt == 1))
            P.tt("dve", scT[:], M[:], mask2[:].rearrange("p r n -> p (r n)"), ALU.mult, [M, mask2], [scT])
            osb = on_sb[c % 2]
            KV = [A[0], A[1], Gk[0], Gk[1]]
            E1, E2 = E[c % 2][1], E[c % 2][2]
            tmpB = tmp_bufs[c % 2]
            for hh in range(4):
                o = O[hh % 2]
                tmpT = E1 if hh < 2 else E2
                tmp = tmpT[:, (hh % 2) * 512:(hh % 2 + 1) * 512]
                for dt in range(2):
                    P.mm(o[:], qtT[:, 2 * hh + dt, :], S16[2 * hh + dt][:], dt == 0, False,
                         [qtT, S16[2 * hh + dt]], [o], signal=False)
                P.mm(o[:], scT[:, hh * 128:(hh + 1) * 128], v_sb[:, hh * 512:(hh + 1) * 512], False, True,
                     [scT, v_sb], [o])
                P.tt("dve", tmp, o[:], gsil[:, hh * 512:(hh + 1) * 512], ALU.mult, [o, gsil], [tmpB[hh]])
                P.act(kt[:, 0:512], o[:], AF.Square, [o], [kt, mso], accum_out=mso[:, hh:hh + 1])
                rstd_from_ms(P, mso[:, hh:hh + 1], rso[:, hh:hh + 1], 512, [mso], [rso], eng="act")
                for dt in range(2):
                    i = 2 * hh + dt
                    a = KV[i % 4]
                    P.mm(a[:], kd[:, i * 128:(i + 1) * 128], v_sb[:, hh * 512:(hh + 1) * 512], True, True,
                         [kd, v_sb], [a])
                    P.stt(S32[i][:], S32[i][:], eb[:, i:i + 1], a[:], ALU.mult, ALU.add, [S32[i], eb, a], [S32[i]])
                    P.copy("pool", S16[i][:], S32[i][:], [S32[i]], [S16[i]])
                P.act(osb[:, hh * 512:(hh + 1) * 512], tmp, AF.Copy, [tmpB[hh], tmpT, rso], [osb],
                      scale=rso[:, hh:hh + 1])
            P.dma("pool", ("on", c % 2), on_d[c * 128:(c + 1) * 128, :], osb[:], reads=[osb], writes=[on_bufs[c]])

        for step in range(NCH + 4):
            c3, c4 = step - 3, step - 4
            v3, v4 = 0 <= c3 < NCH, 0 <= c4 < NCH
            if v3:
                s3a_1(c3)
            if v4:
                s4_gate(c4)
            if v3:
                s3a_2(c3)
            if v4:
                s4_v(c4)
            if v3:
                s3a_3(c3)
            if v4:
                s4_q(c4)
                s4_k(c4)
            if v3:
                s3b(c3)
            if v4:
                s4_tr(c4)
                s4_rec(c4)
            for k, fn in ((2, s2), (1, s1), (0, s0)):
                c = step - k
                if 0 <= c < NCH:
                    fn(c)
        Z.end()

    def phase_b(li, h_src, w_out_d, final):
        Z = Phase(nc, P, "b%d" % li)
        w_out = Z.sb("w_out", [128, 16, D], BF16)
        wg = Z.sb("wg", [128, 8, D], BF16)
        wp = Z.sb("wp", [128, 2, D], BF16)
        g_ple = Z.sb("g_ple", [128, D], F32)
        hx = [Z.sb("hx%d" % i, [128, D], F32) for i in range(2)]
        on_sb = [Z.sb("on%d" % i, [128, MIX], BF16) for i in range(2)]
        p_sb = [Z.sb("p%d" % i, [128, PLE], BF16) for i in range(2)]
        onTa = [Z.sb("onTa%d" % i, [128, 8, 128], BF16) for i in range(2)]
        onTb = [Z.sb("onTb%d" % i, [128, 8, 128], BF16) for i in range(2)]
        pT = [Z.sb("pT%d" % i, [128, 2, 128], BF16) for i in range(2)]
        junk = Z.sb("junk", [128, D], BF16)
        h1 = [Z.sb("h1_%d" % i, [128, D], F32) for i in range(3)]
        hn2 = [Z.sb("hn2_%d" % i, [128, D], BF16) for i in range(2)]
        hn2T = [Z.sb("hn2T%d" % i, [128, 8, 128], BF16) for i in range(2)]
        sg = Z.sb("sg", [128, D], F32)
        h2 = [Z.sb("h2_%d" % i, [128, D], F32) for i in range(2)]
        ms = [Z.sb("ms%d" % i, [128, 2], F32) for i in range(4)]
        rstd = [Z.sb("rstd%d" % i, [128, 2], F32) for i in range(4)]
        T0 = Z.ps("T0", [128, 1024], BF16)
        T1 = Z.ps("T1", [128, 1024], BF16)
        T2 = Z.ps("T2", [128, 1024], BF16)
        Y = [Z.ps("Y%d" % i, [128, 512]) for i in range(2)]
        R = [Z.ps("R%d" % i, [128, 512]) for i in range(3)]
        if final:
            g_fin = Z.sb("g_fin", [128, D], F32)
            o_sb = [Z.sb("o%d" % i, [128, D], F32) for i in range(2)]

        woB = [Buf("w_out_a"), Buf("w_out_b")]
        for nb_ in range(2):
            P.dma_group("pool", ("w", 0 if nb_ == 0 else 7),
                        [(w_out[:, k, nb_ * 512:(nb_ + 1) * 512], w_out_d[k * 128:(k + 1) * 128, nb_ * 512:(nb_ + 1) * 512])
                         for k in range(16)], writes=[woB[nb_]])
        P.dma_group("pool", ("w", 1), [(wg[:, k, :], ple_w_gate[li, k * 128:(k + 1) * 128, :]) for k in range(8)],
                    writes=[wg])
        P.dma_group("pool", ("w", 2), [(wp[:, k, :], ple_w_proj[li, k * 128:(k + 1) * 128, :]) for k in range(2)],
                    writes=[wp])
        P.dma("sp", ("w", 3), g_ple[:], ple_norm[li].partition_broadcast(128), writes=[g_ple])
        if final:
            P.dma("sp", ("w", 6), g_fin[:], final_norm.partition_broadcast(128), writes=[g_fin])

        r_rot = [0]

        def nextR():
            r_rot[0] = (r_rot[0] + 1) % 3
            return R[r_rot[0]]

        def s0(c):
            s = c % 2
            P.dma("sp", ("onl", s), on_sb[s][:], on_d[c * 128:(c + 1) * 128, :], reads=[on_bufs[c]], writes=[on_sb[s]])

        def s1(c):
            s = c % 2
            oc = on_sb[s]
            P.dma("sp", ("hx", s), hx[s][:], h_src[c * 128:(c + 1) * 128, :], reads=[h_bufs[c]], writes=[hx[s]])
            for k in range(16):
                T = T0 if k < 8 else T1
                P.tr(T[:, (k % 8) * 128:(k % 8 + 1) * 128], oc[:, k * 128:(k + 1) * 128], ident[:], [oc, ident], [T],
                     signal=(k % 8 == 7))
            P.copy("dve", onTa[s][:].rearrange("p k n -> p (k n)"), T0[:], [T0], [onTa[s]])
            P.copy("act", onTb[s][:].rearrange("p k n -> p (k n)"), T1[:], [T1], [onTb[s]])

        def s2(c):
            s = c % 2
            hc = hx[s]
            h1c = h1[c % 3]
            m, rs = ms[c % 4], rstd[c % 4]
            P.dma("pool", ("pl", s), p_sb[s][:], p_in[li, c * 128:(c + 1) * 128, :], writes=[p_sb[s]])
            for nb in range(2):
                y = Y[nb]
                for k in range(16):
                    oT = onTa[s] if k < 8 else onTb[s]
                    P.mm(y[:], oT[:, k % 8, :], w_out[:, k, nb * 512:(nb + 1) * 512], k == 0, k == 15,
                         [oT, woB[nb]], [y])
                P.tt("dve", h1c[:, nb * 512:(nb + 1) * 512], y[:], hc[:, nb * 512:(nb + 1) * 512], ALU.add,
                     [y, hc], [h1c])
            P.act(junk[:], h1c[:], AF.Square, [h1c], [junk, m], accum_out=m[:, 0:1])
            rstd_from_ms(P, m[:, 0:1], rs[:, 0:1], D, [m], [rs])
            P.stt(hn2[s][:], h1c[:], rs[:, 0:1], g_ple[:], ALU.mult, ALU.mult, [h1c, rs, g_ple], [hn2[s]])

        def s3(c):
            s = c % 2
            for k in range(8):
                P.tr(T2[:, k * 128:(k + 1) * 128], hn2[s][:, k * 128:(k + 1) * 128], ident[:], [hn2[s], ident], [T2],
                     signal=(k == 7))
            P.copy("dve", hn2T[s][:].rearrange("p k n -> p (k n)"), T2[:], [T2], [hn2T[s]])
            pc = p_sb[s]
            for k in range(2):
                P.tr(T0[:, k * 128:(k + 1) * 128], pc[:, k * 128:(k + 1) * 128], ident[:], [pc, ident], [T0],
                     signal=(k == 1))
            P.copy("act", pT[s][:].rearrange("p k n -> p (k n)"), T0[:, 0:256], [T0], [pT[s]])

        def s4(c):
            s = c % 2
            h1c = h1[c % 3]
            hout = h2[s]
            m, rs = ms[c % 4], rstd[c % 4]
            for nb in range(2):
                gb = nextR()
                for k in range(8):
                    P.mm(gb[:], hn2T[s][:, k, :], wg[:, k, nb * 512:(nb + 1) * 512], k == 0, k == 7,
                         [hn2T[s], wg], [gb])
                P.act(sg[:, nb * 512:(nb + 1) * 512], gb[:], AF.Sigmoid, [gb], [sg])
                pb = nextR()
                for k in range(2):
                    P.mm(pb[:], pT[s][:, k, :], wp[:, k, nb * 512:(nb + 1) * 512], k == 0, k == 1, [pT[s], wp], [pb])
                P.tt("dve", sg[:, nb * 512:(nb + 1) * 512], pb[:], sg[:, nb * 512:(nb + 1) * 512], ALU.mult,
                     [pb, sg], [sg])
                P.tt("dve", hout[:, nb * 512:(nb + 1) * 512], sg[:, nb * 512:(nb + 1) * 512],
                     h1c[:, nb * 512:(nb + 1) * 512], ALU.add, [sg, h1c], [hout])
            if final:
                P.act(junk[:], hout[:], AF.Square, [hout], [junk, m], accum_out=m[:, 1:2])
                rstd_from_ms(P, m[:, 1:2], rs[:, 1:2], D, [m], [rs])
                P.stt(o_sb[s][:], hout[:], rs[:, 1:2], g_fin[:], ALU.mult, ALU.mult, [hout, rs, g_fin], [o_sb[s]])
                P.dma("sp", ("os", s), out[c * 128:(c + 1) * 128, :], o_sb[s][:], reads=[o_sb[s]], writes=[out_bufs[c]])
            else:
                P.dma("sp", ("hs", s), hbuf[c * 128:(c + 1) * 128, :], hout[:], reads=[hout], writes=[h_bufs[c]])

        run_pipeline([s0, s1, s2, s3, s4], NCH)
        Z.end()

    def phase_kv():
        Z = Phase(nc, P, "kv")
        wkv = Z.sb("wkv", [128, 8, 4096], BF16)
        g_kv = Z.sb("g_kv", [128, D], F32)
        hx = [Z.sb("hx%d" % i, [128, D], F32) for i in range(2)]
        junk = Z.sb("junk", [128, D], BF16)
        kvn = [Z.sb("kvn%d" % i, [128, D], BF16) for i in range(2)]
        kvnT = [Z.sb("kvnT%d" % i, [128, 8, 128], BF16) for i in range(2)]
        kT_sb = [Z.sb("kT%d" % i, [128, 16, 128], BF16) for i in range(2)]
        v_o = [Z.sb("vo%d" % i, [128, MIX], BF16) for i in range(2)]
        ms = [Z.sb("ms%d" % i, [128, 2], F32) for i in range(4)]
        rstd = [Z.sb("rstd%d" % i, [128, 2], F32) for i in range(4)]
        T0 = Z.ps("T0", [128, 1024], BF16)
        Bk = [Z.ps("B%d" % i, [128, 512]) for i in range(6)]
        wkvK, wkvV = Buf("wkvK"), Buf("wkvV")
        P.dma_group("pool", ("w", 4), [(wkv[:, k, 0:2048], w_kv[k * 128:(k + 1) * 128, 0:2048]) for k in range(8)],
                    writes=[wkvK], max_dma_last_dim=4096)
        P.dma_group("pool", ("w", 6), [(wkv[:, k, 2048:4096], w_kv[k * 128:(k + 1) * 128, 2048:4096])
                                       for k in range(8)], writes=[wkvV], max_dma_last_dim=4096)
        P.dma("sp", ("w", 5), g_kv[:], kv_norm.partition_broadcast(128), writes=[g_kv])
        b_rot = [0]

        def nextB():
            b_rot[0] = (b_rot[0] + 1) % 6
            return Bk[b_rot[0]]

        def s0(c):
            s = c % 2
            P.dma("sp", ("hx", s), hx[s][:], hbuf[c * 128:(c + 1) * 128, :], reads=[h_bufs[c]], writes=[hx[s]])

        def s1(c):
            s = c % 2
            m, rs = ms[c % 4], rstd[c % 4]
            P.act(junk[:], hx[s][:], AF.Square, [hx[s]], [junk, m], accum_out=m[:, 0:1])
            rstd_from_ms(P, m[:, 0:1], rs[:, 0:1], D, [m], [rs])
            P.stt(kvn[s][:], hx[s][:], rs[:, 0:1], g_kv[:], ALU.mult, ALU.mult, [hx[s], rs, g_kv], [kvn[s]])

        def s2(c):
            s = c % 2
            for k in range(8):
                P.tr(T0[:, k * 128:(k + 1) * 128], kvn[s][:, k * 128:(k + 1) * 128], ident[:], [kvn[s], ident], [T0],
                     signal=(k == 7))
            P.copy("dve", kvnT[s][:].rearrange("p k n -> p (k n)"), T0[:], [T0], [kvnT[s]])

        def s3(c):
            s = c % 2
            kts = kT_sb[s]
            for cg in range(4):
                kb = nextB()
                for ci in range(4):
                    ct = cg * 4 + ci
                    for k in range(8):
                        P.mm(kb[:, ci * 128:(ci + 1) * 128], wkv[:, k, ct * 128:(ct + 1) * 128], kvnT[s][:, k, :],
                             k == 0, k == 7, [wkvK, kvnT[s]], [kb], signal=(k == 7 and ci == 3))
                P.copy("act" if cg % 2 else "dve", kts[:, cg * 4:(cg + 1) * 4, :].rearrange("p k n -> p (k n)"),
                       kb[:], [kb], [kts])
            P.dma("pool", ("ks", s), kT_d[:, :, c * 128:(c + 1) * 128].rearrange("t d n -> d t n"), kts[:],
                  reads=[kts], writes=[kv_buf])
            vo = v_o[s]
            for nb in range(4):
                vb = nextB()
                for k in range(8):
                    P.mm(vb[:], kvnT[s][:, k, :], wkv[:, k, 2048 + nb * 512:2048 + (nb + 1) * 512], k == 0, k == 7,
                         [kvnT[s], wkvV], [vb])
                P.copy("act" if nb % 2 else "dve", vo[:, nb * 512:(nb + 1) * 512], vb[:], [vb], [vo])
            P.dma("pool", ("vs", s), v_d[c * 128:(c + 1) * 128, :], vo[:], reads=[vo], writes=[kv_buf])

        run_pipeline([s0, s1, s2, s3], NCH)
        Z.end()

    def phase_diff(li):
        j = li - 2
        lam_init = 0.8 - 0.6 * math.exp(-0.3 * li)
        Z = Phase(nc, P, "d%d" % li)
        hnT = Z.sb("hnT", [128, 8, S], BF16)
        g_mix = Z.sb("g_mix", [128, D], F32)
        gd = Z.sb("gd", [128, 256], F32)
        hx = [Z.sb("hx%d" % i, [128, D], F32) for i in range(2)]
        ms = Z.sb("ms", [128, 4], F32)
        rstd = Z.sb("rstd", [128, 4], F32)
        lam4 = Z.sb("lam4", [128, 4], F32)
        prods = Z.sb("prods", [128, 2], F32)
        lam = Z.sb("lam", [128, 4], F32)
        wq = [Z.sb("wq%d" % i, [128, 8, 256], BF16) for i in range(2)]
        wgt = [Z.sb("wgt%d" % i, [128, 8, 256], BF16) for i in range(2)]
        kTh = [Z.sb("kTh%d" % i, [128, 2, S], BF16) for i in range(2)]
        vh = [Z.sb("vh%d" % i, [128, NCH, 258], BF16) for i in range(2)]
        qTh = Z.sb("qTh", [128, 2, S], BF16)
        PT = [Z.sb("PT%d" % i, [128, 512], BF16) for i in range(4)]
        o_sb = [Z.sb("o%d" % i, [128, 256], F32) for i in range(2)]
        oraw = [Z.sb("or%d" % q, [128, 2, 258], F32) for q in range(2)]
        sil_h = Z.sb("sil_h", [128, NCH, 256], BF16)
        on_o = [Z.sb("ono%d" % i, [128, 256], BF16) for i in range(2)]
        rr = Z.sb("rr", [128, 4], F32)
        Sb = [Z.ps("S%d" % i, [128, 512]) for i in range(4)]
        T0 = TB(Sb[3].t.bitcast(BF16), "T0")
        T0.b = Sb[3].b
        Oa = [[Z.ps("O%d%d" % (q, t), [128, 512]) for t in range(2)] for q in range(2)]
        Gb = Sb[2]

        P.dma("sp", ("w", 0), g_mix[:], norm_mix[li].partition_broadcast(128), writes=[g_mix])
        P.dma("sp", ("w", 1), gd[:], diff_norm[j].partition_broadcast(128), writes=[gd])
        P.ts("dve", gd[:], gd[:], 1.0 - lam_init, None, ALU.mult, None, [gd], [gd])
        P.dma("sp", ("w", 2), lam4[:], diff_lambda[j].rearrange("f d -> d f"), writes=[lam4],
              allow_slow_non_contiguous=True)
        P.tt("dve", prods[:, 0:1], lam4[:, 0:1], lam4[:, 1:2], ALU.mult, [lam4], [prods])
        P.tt("dve", prods[:, 1:2], lam4[:, 2:3], lam4[:, 3:4], ALU.mult, [lam4], [prods])
        P.mm(Gb[:, 0:2], ones32[:], prods[:], True, True, [ones32, prods], [Gb])
        P.act(lam[:, 0:2], Gb[:, 0:2], AF.Exp, [Gb], [lam])
        P.tt("dve", lam[:, 2:3], lam[:, 1:2], lam[:, 0:1], ALU.subtract, [lam], [lam])
        P.ts("dve", lam[:, 2:3], lam[:, 2:3], -lam_init, None, ALU.add, None, [lam], [lam])

        for i in range(2):
            P.memset("pool", vh[i][:, :, 256:258], 1.0, [vh[i]])

        def head_loads(hd):
            s = hd % 2
            P.dma_group("pool", ("wq", s), [(wq[s][:, k, :],
                                             diff_w_in[j, k * 128:(k + 1) * 128, hd * 256:(hd + 1) * 256])
                                            for k in range(8)], writes=[wq[s]])
            P.dma_group("pool", ("wg", s), [(wgt[s][:, k, :],
                                             diff_w_in[j, k * 128:(k + 1) * 128, 2048 + hd * 256:2048 + (hd + 1) * 256])
                                            for k in range(8)], writes=[wgt[s]])
            P.dma("sp", ("kh", s), kTh[s][:], kT_d[2 * hd:2 * hd + 2].rearrange("t d n -> d t n"),
                  reads=[kv_buf], writes=[kTh[s]])
            P.dma_group("sp", ("vh", s), [(vh[s][:, g * 8:(g + 1) * 8, 0:256],
                                           v_d[g * 1024:(g + 1) * 1024, hd * 256:(hd + 1) * 256].rearrange(
                                               "(n p) c -> p n c", p=128)) for g in range(4)],
                        reads=[kv_buf], writes=[vh[s]])

        head_loads(0)
        msP = [Z.sb("msP%d" % i, [128, 2], F32) for i in range(2)]
        rsP = [Z.sb("rsP%d" % i, [128, 2], F32) for i in range(2)]
        hnS = [qTh[:, i, 0:1024] for i in range(2)]
        hnB = [Buf("hnS0"), Buf("hnS1")]
        jnk = qTh[:, 0, 2048:3072]
        jnkB = Buf("jnk")

        def p0(c):
            P.dma("sp", ("hx", c % 2), hx[c % 2][:], hbuf[c * 128:(c + 1) * 128, :],
                  reads=[h_bufs[c]], writes=[hx[c % 2]])

        def p1(c):
            hc, m, rs = hx[c % 2], msP[c % 2], rsP[c % 2]
            P.act(jnk, hc[:], AF.Square, [hc], [jnkB, m], accum_out=m[:, 0:1])
            rstd_from_ms(P, m[:, 0:1], rs[:, 0:1], D, [m], [rs])
            P.stt(hnS[c % 2], hc[:], rs[:, 0:1], g_mix[:], ALU.mult, ALU.mult, [hc, rs, g_mix], [hnB[c % 2]])

        def p2(c):
            src = hnS[c % 2]
            for k in range(8):
                P.tr(T0[:, k * 128:(k + 1) * 128], src[:, k * 128:(k + 1) * 128], ident[:], [hnB[c % 2], ident], [T0],
                     signal=(k == 7))
            P.copy("dve", hnT[:, :, c * 128:(c + 1) * 128], T0[:].rearrange("p (k n) -> p k n", k=8), [T0], [hnT])

        run_pipeline([p0, p1, p2], NCH)

        scale = 128 ** -0.5
        LA = 3
        silB = [Buf("sil%d" % i) for i in range(16)]
        for hd in range(8):
            s = hd % 2
            if hd + 1 < 8:
                head_loads(hd + 1)
            for t in range(2):
                for g in range(8):
                    sb_ = Sb[g % 3]
                    for k in range(8):
                        P.mm(sb_[:], wq[s][:, k, t * 128:(t + 1) * 128], hnT[:, k, g * 512:(g + 1) * 512],
                             k == 0, k == 7, [wq[s], hnT], [sb_])
                    ev = "act" if (g % 2 or (t == 0 and g < 4)) else "dve"
                    P.copy(ev, qTh[:, t, g * 512:(g + 1) * 512], sb_[:], [sb_], [qTh])
            for pr in range(16):
                sb_ = Sb[pr % 3]
                for q in range(2):
                    qt_i = 2 * pr + q
                    for k in range(8):
                        P.mm(sb_[:, q * 256:(q + 1) * 256], hnT[:, k, qt_i * 128:(qt_i + 1) * 128], wgt[s][:, k, :],
                             k == 0, k == 7, [hnT, wgt[s]], [sb_], signal=(k == 7 and q == 1))
                P.act(sil_h[:, 2 * pr:2 * pr + 2, :].rearrange("p a c -> p (a c)"), sb_[:], AF.Silu, [sb_], [silB[pr]])
                for q in range(2):
                    P.tt("pool", sil_h[:, 2 * pr + q, :], sil_h[:, 2 * pr + q, :], gd[:], ALU.mult,
                         [silB[pr], gd], [silB[pr]])

            items = [(r, kt_) for r in range(16) for kt_ in range(2 * r + 2)]

            def emit_ST(i):
                r, kt_ = items[i]
                sb_ = Sb[i % 4]
                if kt_ == 2 * r + 1:
                    for t in range(2):
                        reg = sb_[:, t * 256 + 128:(t + 1) * 256]
                        P.mm(reg, kTh[s][:, t, kt_ * 128:(kt_ + 1) * 128], qTh[:, t, r * 256 + 128:(r + 1) * 256],
                             True, False, [kTh[s], qTh], [sb_], signal=False)
                        P.mm(reg, ident[:], maskb[:], False, True, [ident, maskb], [sb_], signal=(t == 1))
                else:
                    for t in range(2):
                        dg = (kt_ == 2 * r)
                        P.mm(sb_[:, t * 256:(t + 1) * 256], kTh[s][:, t, kt_ * 128:(kt_ + 1) * 128],
                             qTh[:, t, r * 256:(r + 1) * 256], True, not dg, [kTh[s], qTh], [sb_],
                             signal=(t == 1 and not dg))
                        if dg:
                            P.mm(sb_[:, t * 256:t * 256 + 128], ident[:], maskb[:], False, True, [ident, maskb], [sb_],
                                 signal=(t == 1))

            def emit_exp(i):
                r, kt_ = items[i]
                sb_ = Sb[i % 4]
                pt = PT[i % 4]
                if kt_ != 2 * r + 1:
                    P.act(pt[:], sb_[:], AF.Exp, [sb_], [pt], scale=scale)
                else:
                    v3 = sb_[:].rearrange("p (t q) -> p t q", t=2)[:, :, 128:256]
                    p3 = pt[:].rearrange("p (t q) -> p t q", t=2)[:, :, 128:256]
                    P.act(p3, v3, AF.Exp, [sb_], [pt], scale=scale)

            def emit_PV(i):
                r, kt_ = items[i]
                pt = PT[i % 4]
                qtiles = (0, 1) if kt_ != 2 * r + 1 else (1,)
                for q in qtiles:
                    last = (kt_ == 2 * r + q)
                    for t in range(2):
                        P.mm(Oa[q][t][:, 0:257], pt[:, t * 256 + q * 128:t * 256 + (q + 1) * 128],
                             vh[s][:, kt_, 0:257], kt_ == 0, last, [pt, vh[s]], [Oa[q][t]],
                             signal=(last or t == 1))

            def fin_A1(r, q):
                O1, O2 = Oa[q][0], Oa[q][1]
                rw = oraw[q]
                P.copy("dve", rw[:, 0, 0:257], O1[:, 0:257], [O1], [rw])
                P.copy("dve", rw[:, 1, 0:257], O2[:, 0:257], [O2], [rw])

            def fin_A2(r, q):
                o_ = o_sb[q]
                rw = oraw[q]
                rq = rr[:, 2 * q:2 * q + 2]
                P.recip(rq, rw[:, :, 256], [rw], [rr])
                P.ts("dve", o_[:], rw[:, 1, 0:256], rr[:, 2 * q + 1:2 * q + 2], lam[:, 2:3], ALU.mult, ALU.mult,
                     [rw, rr, lam], [o_])
                P.stt(o_[:], rw[:, 0, 0:256], rr[:, 2 * q:2 * q + 1], o_[:], ALU.mult, ALU.add, [rw, rr, o_], [o_])
                P.op("dve", lambda h, o_=o_, q=q: h.scalar_tensor_tensor(
                    out=on_o[q][:], in0=o_[:], scalar=1.0, in1=o_[:], op0=ALU.mult, op1=ALU.mult,
                    accum_out=ms[:, 1 + q:2 + q]), [o_], [on_o[q], ms])

            def fin_B(r, q):
                qt_i = 2 * r + q
                o_ = o_sb[q]
                rstd_from_ms(P, ms[:, 1 + q:2 + q], rstd[:, 1 + q:2 + q], 256, [ms], [rstd])
                oo = on_o[q]
                P.stt(oo[:], o_[:], rstd[:, 1 + q:2 + q], sil_h[:, qt_i, :], ALU.mult, ALU.mult,
                      [o_, rstd, silB[qt_i // 2]], [oo])
                P.dma("pool", ("ono", q), on_d[qt_i * 128:(qt_i + 1) * 128, hd * 256:(hd + 1) * 256], oo[:],
                      reads=[oo], writes=[on_bufs[qt_i]])

            deferred = []
            n_it = len(items)
            for i in range(min(LA, n_it)):
                emit_ST(i)
            for i in range(n_it):
                r, kt_ = items[i]
                if i + LA < n_it:
                    emit_ST(i + LA)
                emit_exp(i)
                emit_PV(i)
                for (due, fn) in [d for d in deferred if d[0] <= i]:
                    fn()
                deferred = [d for d in deferred if d[0] > i]
                if kt_ == 2 * r:
                    fin_A1(r, 0)
                if kt_ == 2 * r + 1:
                    fin_A1(r, 1)
                    fin_A2(r, 0)
                    fin_A2(r, 1)
                    deferred.append((i + 2, lambda r=r: fin_B(r, 0)))
                    deferred.append((i + 2, lambda r=r: fin_B(r, 1)))
            for (due, fn) in deferred:
                fn()
        Z.end()

    P.barrier()
    for ph in phases:
        kind, li = ph[0], int(ph[1])
        if kind == "g":
            phase_gla(li, x if li == 0 else hbuf)
        elif kind == "b":
            wo = gla_w_out[li] if li < 2 else diff_w_out[li - 2]
            phase_b(li, x if li == 0 else hbuf, wo, final=(li == 3))
            if li == 1:
                phase_kv()
        elif kind == "d":
            phase_diff(li)
    G.es.close()
    P.close()
    return nc, P


_CACHE = {}


def _in_maps(inputs):
    maps = []
    shared = {k: np.ascontiguousarray(v, dtype=np.float32) for k, v in inputs.items() if k not in ("x", "p")}
    for b in range(N_CORES):
        m = dict(shared)
        m["x"] = np.ascontiguousarray(inputs["x"][b], dtype=np.float32)
        m["p"] = np.ascontiguousarray(inputs["p"][:, b], dtype=np.float32)
        maps.append(m)
    return maps


def kernel(**inputs):
    if "nc" not in _CACHE:
        _CACHE["nc"] = build_program()[0]
    nc = _CACHE["nc"]
    res = run_bass_kernel_spmd(nc, _in_maps(inputs), core_ids=list(range(N_CORES)))
    return np.stack([np.asarray(r["out"], dtype=np.float32) for r in res.results], axis=0)
```

```python
import math
from contextlib import ExitStack

import numpy as np
import concourse.bass as bass
import concourse.mybir as mybir
from concourse.bass_utils import run_bass_kernel_spmd

F32 = mybir.dt.float32
BF16 = mybir.dt.bfloat16
AF = mybir.ActivationFunctionType
ALU = mybir.AluOpType

S = 4096
D = 1024
NCH = S // 128
MIX = 2048
PLE = 256
EPS = 1e-6
GLA_IN = 6160
N_CORES = 8

SAME_ENGINE_SYNC = True


class Buf:
    __slots__ = ("name", "last_write", "reads", "psum")

    def __init__(self, name=""):
        self.name = name
        self.last_write = None
        self.reads = []
        self.psum = False


class TB:
    def __init__(self, t, name):
        self.t = t
        self.b = Buf(name)

    def __getitem__(self, k):
        return self.t[k]


class Eng:
    def __init__(self, name, sem):
        self.name = name
        self.sem = sem
        self.count = 0
        self.ops = []
        self.waited = {}
        self.pending = False


class Prog:
    def __init__(self, nc):
        self.nc = nc
        self.es = ExitStack()
        self.engs = {}
        for n in ("pe", "act", "dve", "pool", "sp"):
            sem = self.es.enter_context(nc.semaphore("sem_" + n))
            self.engs[n] = Eng(n, sem)
        self.dma_sems = {}
        self.n_ops = 0

    def dma_sem(self, key):
        if key not in self.dma_sems:
            sem = self.es.enter_context(self.nc.semaphore("dsem_%d" % len(self.dma_sems)))
            self.dma_sems[key] = [sem, 0]
        return self.dma_sems[key]

    def _deps(self, reads, writes):
        deps = []
        for b in reads:
            if b.last_write is not None:
                deps.append(b.last_write)
            if b.psum:
                deps.extend(b.reads)
        for b in writes:
            if b.last_write is not None:
                deps.append(b.last_write)
            deps.extend(b.reads)
        return deps

    def _emit_waits(self, e, deps):
        need = {}
        for (sem, val) in deps:
            if sem is e.sem:
                if e.name == "pe" or not SAME_ENGINE_SYNC:
                    continue
                if val > e.count:
                    continue
            k = id(sem)
            if e.waited.get(k, 0) >= val:
                continue
            if k not in need or need[k][1] < val:
                need[k] = (sem, val)
        for k, (sem, val) in need.items():
            e.waited[k] = val
            e.ops.append(("wait", sem, val))

    def _check_pending(self, e):
        for o in self.engs.values():
            if o is not e and o.pending:
                raise RuntimeError("op on %s while %s has an unsignalled group open" % (e.name, o.name))

    def op(self, eng, fn, reads=(), writes=(), signal=True):
        e = self.engs[eng]
        self._check_pending(e)
        reads = [r.b if isinstance(r, TB) else r for r in reads]
        writes = [w.b if isinstance(w, TB) else w for w in writes]
        self._emit_waits(e, self._deps(reads, writes))
        self.n_ops += 1
        if signal:
            e.count += 1
            tok = (e.sem, e.count)
            e.ops.append(("op", fn, e.sem, 1))
            e.pending = False
        else:
            tok = (e.sem, e.count + 1)
            e.ops.append(("op", fn, None, 0))
            e.pending = True
        for b in reads:
            b.reads.append(tok)
        for b in writes:
            b.last_write = tok
            b.reads = []
        return tok

    def dma(self, queue, key, out, in_, reads=(), writes=(), **kw):
        e = self.engs[queue]
        self._check_pending(e)
        assert not e.pending
        reads = [r.b if isinstance(r, TB) else r for r in reads]
        writes = [w.b if isinstance(w, TB) else w for w in writes]
        self._emit_waits(e, self._deps(reads, writes))
        ds = self.dma_sem(key)
        ds[1] += 16
        tok = (ds[0], ds[1])
        e.ops.append(("op", lambda h, o=out, i=in_, kw=kw: h.dma_start(out=o, in_=i, **kw), ds[0], 16))
        self.n_ops += 1
        for b in reads:
            b.reads.append(tok)
        for b in writes:
            b.last_write = tok
            b.reads = []
        return tok

    def dma_group(self, queue, key, pairs, reads=(), writes=(), **kw):
        e = self.engs[queue]
        self._check_pending(e)
        reads = [r.b if isinstance(r, TB) else r for r in reads]
        writes = [w.b if isinstance(w, TB) else w for w in writes]
        self._emit_waits(e, self._deps(reads, writes))
        ds = self.dma_sem(key)
        for (out, in_) in pairs:
            ds[1] += 16
            e.ops.append(("op", lambda h, o=out, i=in_, kw=kw: h.dma_start(out=o, in_=i, **kw), ds[0], 16))
            self.n_ops += 1
        tok = (ds[0], ds[1])
        for b in reads:
            b.reads.append(tok)
        for b in writes:
            b.last_write = tok
            b.reads = []
        return tok

    def barrier(self):
        for e in self.engs.values():
            assert not e.pending
        toks = [(e.sem, e.count) for e in self.engs.values() if e.count > 0]
        toks += [(s, v) for (s, v) in self.dma_sems.values() if v > 0]
        for e in self.engs.values():
            self._emit_waits(e, [t for t in toks if t[0] is not e.sem])

    def emit_block(self):
        nc = self.nc
        for e in self.engs.values():
            assert not e.pending, e.name

        def replay(h, e):
            for rec in e.ops:
                if rec[0] == "wait":
                    h.wait_ge(rec[1], rec[2])
                else:
                    ins = rec[1](h)
                    if rec[2] is not None:
                        ins.then_inc(rec[2], rec[3])
            e.ops = []

        with nc.Block() as block:
            @block.tensor
            def _(h):
                replay(h, self.engs["pe"])

            @block.scalar
            def _(h):
                replay(h, self.engs["act"])

            @block.vector
            def _(h):
                replay(h, self.engs["dve"])

            @block.gpsimd
            def _(h):
                replay(h, self.engs["pool"])

            @block.sync
            def _(h):
                replay(h, self.engs["sp"])

    def close(self):
        self.es.close()

    def mm(self, out, lhsT, rhs, start, stop, reads, writes, signal=None):
        if signal is None:
            signal = stop
        return self.op("pe", lambda h: h.matmul(out, lhsT=lhsT, rhs=rhs, start=start, stop=stop),
                       reads, writes, signal=signal)

    def tr(self, out, in_, ident, reads, writes, signal=True):
        return self.op("pe", lambda h: h.transpose(out, in_, ident), reads, writes, signal=signal)

    def act(self, out, in_, func, reads, writes, eng="act", **kw):
        return self.op(eng, lambda h: h.activation(out=out, in_=in_, func=func, **kw), reads, writes)

    def tt(self, eng, out, in0, in1, op, reads, writes):
        return self.op(eng, lambda h: h.tensor_tensor(out=out, in0=in0, in1=in1, op=op), reads, writes)

    def ts(self, eng, out, in0, s1, s2, op0, op1, reads, writes):
        if s2 is None:
            return self.op(eng, lambda h: h.tensor_scalar(out=out, in0=in0, scalar1=s1, scalar2=None, op0=op0),
                           reads, writes)
        return self.op(eng, lambda h: h.tensor_scalar(out=out, in0=in0, scalar1=s1, scalar2=s2, op0=op0, op1=op1),
                       reads, writes)

    def stt(self, out, in0, scalar, in1, op0, op1, reads, writes):
        return self.op("dve", lambda h: h.scalar_tensor_tensor(out=out, in0=in0, scalar=scalar, in1=in1,
                                                                 op0=op0, op1=op1), reads, writes)

    def copy(self, eng, out, in_, reads, writes):
        if eng == "act":
            return self.op("act", lambda h: h.activation(out=out, in_=in_, func=AF.Copy), reads, writes)
        return self.op(eng, lambda h: h.tensor_copy(out=out, in_=in_), reads, writes)

    def memset(self, eng, ap, val, writes):
        return self.op(eng, lambda h: h.memset(ap, val), [], writes)

    def recip(self, out, in_, reads, writes):
        return self.op("dve", lambda h: h.reciprocal(out=out, in_=in_), reads, writes)


class Phase:
    def __init__(self, nc, P, name):
        self.nc, self.P, self.name = nc, P, name
        self.es = ExitStack()
        self.k = 0

    def sb(self, name, shape, dtype):
        self.k += 1
        t = self.es.enter_context(self.nc.sbuf_tensor("%s_%s" % (self.name, name), list(shape), dtype))
        return TB(t, name)

    def ps(self, name, shape, dtype=F32):
        t = self.es.enter_context(self.nc.psum_tensor("%s_%s" % (self.name, name), list(shape), dtype))
        tb = TB(t, name)
        tb.b.psum = True
        return tb

    def end(self):
        self.P.barrier()
        self.P.emit_block()
        self.es.close()


def rstd_from_ms(P, ms_ap, out_ap, n, reads, writes, eng="pool"):
    if eng == "act":
        P.act(out_ap, ms_ap, AF.Ln, reads, writes, scale=1.0 / n, bias=EPS)
        P.act(out_ap, out_ap, AF.Exp, writes, writes, scale=-0.5)
        return
    nh = P.nhalf
    P.ts("pool", out_ap, ms_ap, 1.0 / n, EPS, ALU.mult, ALU.add, reads, writes)
    P.tt("pool", out_ap, out_ap, nh[:, 0:1], ALU.pow, list(writes) + [nh], writes)


def build_program(debug=False, phases=("g0", "b0", "g1", "b1", "d2", "b2", "d3", "b3")):
    nc = bass.Bass("TRN2", target_bir_lowering=False)

    def din(name, shape):
        return nc.dram_tensor(name, list(shape), F32, kind="ExternalInput").ap()

    x = din("x", [S, D])
    p_in = din("p", [4, S, PLE])
    norm_mix = din("norm_mix", [4, D])
    gla_w_in = din("gla_w_in", [2, D, GLA_IN])
    gla_w_gk2 = din("gla_w_gk2", [2, 16, 1024])
    gla_b_gk = din("gla_b_gk", [2, 1024])
    gla_norm = din("gla_norm", [2, 512])
    gla_w_out = din("gla_w_out", [2, MIX, D])
    kv_norm = din("kv_norm", [D])
    w_kv = din("w_kv", [D, 4096])
    diff_w_in = din("diff_w_in", [2, D, 4096])
    diff_lambda = din("diff_lambda", [2, 4, 128])
    diff_norm = din("diff_norm", [2, 256])
    diff_w_out = din("diff_w_out", [2, MIX, D])
    ple_norm = din("ple_norm", [4, D])
    ple_w_gate = din("ple_w_gate", [4, D, D])
    ple_w_proj = din("ple_w_proj", [4, PLE, D])
    final_norm = din("final_norm", [D])

    out = nc.dram_tensor("out", [S, D], F32, kind="ExternalOutput").ap()
    skind = "ExternalOutput" if debug else "Internal"
    hbuf = nc.dram_tensor("hbuf", [S, D], F32, kind=skind).ap()
    on_d = nc.dram_tensor("on_d", [S, MIX], BF16, kind=skind).ap()
    kT_d = nc.dram_tensor("kT_d", [16, 128, S], BF16, kind=skind).ap()
    v_d = nc.dram_tensor("v_d", [S, MIX], BF16, kind=skind).ap()

    P = Prog(nc)
    h_bufs = [Buf("h%d" % c) for c in range(NCH)]
    on_bufs = [Buf("on%d" % c) for c in range(NCH)]
    out_bufs = [Buf("out%d" % c) for c in range(NCH)]
    kv_buf = Buf("kvd")

    G = Phase(nc, P, "c")
    ident = G.sb("ident", [128, 128], BF16)
    mask2 = G.sb("mask2", [128, 4, 128], BF16)
    U32 = G.sb("U32", [128, 128], F32)
    L32 = G.sb("L32", [128, 128], F32)
    ones32 = G.sb("ones32", [128, 128], F32)
    P.memset("pool", ident[:], 0.0, [ident])
    P.op("pool", lambda h: h.affine_select(out=ident[:], in_=ident[:], pattern=[[-1, 128]],
                                           compare_op=ALU.not_equal, fill=1.0, base=0, channel_multiplier=1),
         [ident], [ident])
    P.memset("pool", mask2[:], 1.0, [mask2])
    for r in range(4):
        P.op("pool", lambda h, r=r: h.affine_select(out=mask2[:, r, :], in_=mask2[:, r, :], pattern=[[1, 128]],
                                                    compare_op=ALU.is_ge, fill=0.0, base=0, channel_multiplier=-1),
             [mask2], [mask2])
    P.memset("pool", U32[:], 1.0, [U32])
    P.op("pool", lambda h: h.affine_select(out=U32[:], in_=U32[:], pattern=[[1, 128]],
                                           compare_op=ALU.is_ge, fill=0.0, base=0, channel_multiplier=-1),
         [U32], [U32])
    P.memset("pool", L32[:], 1.0, [L32])
    P.op("pool", lambda h: h.affine_select(out=L32[:], in_=L32[:], pattern=[[-1, 128]],
                                           compare_op=ALU.is_gt, fill=0.0, base=0, channel_multiplier=1),
         [L32], [L32])
    P.memset("pool", ones32[:], 1.0, [ones32])
    L16 = G.sb("L16", [128, 128], BF16)
    P.memset("pool", L16[:], 1.0, [L16])
    P.op("pool", lambda h: h.affine_select(out=L16[:], in_=L16[:], pattern=[[-1, 128]],
                                           compare_op=ALU.is_gt, fill=0.0, base=0, channel_multiplier=1),
         [L16], [L16])
    ones16 = G.sb("ones16", [128, 2], BF16)
    P.memset("pool", ones16[:], 1.0, [ones16])
    maskb = G.sb("maskb", [128, 128], BF16)
    P.memset("pool", maskb[:], -30000.0, [maskb])
    P.op("pool", lambda h: h.affine_select(out=maskb[:], in_=maskb[:], pattern=[[-1, 128]],
                                           compare_op=ALU.is_gt, fill=0.0, base=0, channel_multiplier=1),
         [maskb], [maskb])
    nhalf = G.sb("nhalf", [128, 2], F32)
    P.memset("pool", nhalf[:], -0.5, [nhalf])
    P.nhalf = nhalf

    def run_pipeline(stages, n):
        ns = len(stages)
        for step in range(n + ns - 1):
            for k in range(ns - 1, -1, -1):
                c = step - k
                if 0 <= c < n:
                    stages[k](c)

    def phase_gla(li, h_src):
        Z = Phase(nc, P, "g%d" % li)
        w_in = Z.sb("w_in", [128, 8, GLA_IN], BF16)
        wgk = Z.sb("wgk", [32, 1024], BF16)
        g_mix = Z.sb("g_mix", [128, D], F32)
        gn = Z.sb("gn", [128, 512], F32)
        hx = [Z.sb("hx%d" % i, [128, D], F32) for i in range(2)]
        hn = [Z.sb("hn%d" % i, [128, D], BF16) for i in range(2)]
        hnT = [Z.sb("hnT%d" % i, [128, 8, 128], BF16) for i in range(3)]
        v_sb = Z.sb("v", [128, MIX], BF16)
        gsil = Z.sb("gsil", [128, MIX], BF16)
        gkl = Z.sb("gkl", [128, 32], BF16)
        gklT = Z.sb("gklT", [32, 128], BF16)
        sp = [Z.sb("sp%d" % i, [128, 1024], F32) for i in range(1)] * 2
        sph = Z.sb("sph", [128, 1024], BF16)
        spl = Z.sb("spl", [128, 1024], BF16)
        E = [[Z.sb("E%d_%d" % (i, k), [128, 1024], F32) for k in range(3)] for i in range(2)]
        ebl = [Z.sb("ebl%d" % i, [128, 8], F32) for i in range(2)]
        qt = Z.sb("qt", [128, 1024], BF16)
        kt = Z.sb("kt", [128, 1024], BF16)
        kd = Z.sb("kd", [128, 1024], BF16)
        qtT = Z.sb("qtT", [128, 8, 128], BF16)
        ktT = Z.sb("ktT", [128, 8, 128], BF16)
        scT = Z.sb("scT", [128, 512], BF16)
        S32 = [Z.sb("S32_%d" % i, [128, 512], F32) for i in range(8)]
        S16 = [Z.sb("S16_%d" % i, [128, 512], BF16) for i in range(8)]
        on_sb = [Z.sb("on%d" % i, [128, MIX], BF16) for i in range(1)] * 2
        ms = [Z.sb("ms%d" % i, [128, 2], F32) for i in range(2)]
        rstd = [Z.sb("rstd%d" % i, [128, 2], F32) for i in range(2)]
        mso = Z.sb("mso", [128, 4], F32)
        rso = Z.sb("rso", [128, 4], F32)
        A = [Z.ps("A%d" % i, [128, 512]) for i in range(2)]
        Tb = Z.ps("T", [128, 1024], BF16)
        Gk = [Z.ps("G%d" % i, [128, 512]) for i in range(2)]
        M = Z.ps("M", [128, 512])
        O = [Z.ps("O%d" % i, [128, 512]) for i in range(2)]
        Tf = TB(Tb.t.bitcast(F32), "Tf")
        Tf.b = Tb.b
        O4 = [O[0], O[1], M, Tf]

        wgrp = [("gk", 6144, 6160, 0), ("gate", 4096, 6144, 4), ("v", 2048, 4096, 5), ("q", 0, 1024, 6),
                ("k", 1024, 2048, 7)]
        wB = {}
        for (nm, c0, c1, key) in wgrp:
            wB[nm] = Buf("w_" + nm)

        def wbuf(col0):
            if col0 >= 6144:
                return wB["gk"]
            if col0 >= 4096:
                return wB["gate"]
            if col0 >= 2048:
                return wB["v"]
            return wB["q"] if col0 < 1024 else wB["k"]

        nm, c0, c1, key = wgrp[0]
        P.dma_group("pool", ("w", key), [(w_in[:, k, c0:c1], gla_w_in[li, k * 128:(k + 1) * 128, c0:c1])
                                         for k in range(8)], writes=[wB[nm]])
        P.dma_group("pool", ("w", 1), [(wgk[0:16, :], gla_w_gk2[li]), (wgk[16:17, :], gla_b_gk[li:li + 1, :])],
                    writes=[wgk])
        P.dma("sp", ("w", 2), g_mix[:], norm_mix[li].partition_broadcast(128), writes=[g_mix])
        P.dma("sp", ("w", 3), gn[:], gla_norm[li].partition_broadcast(128), writes=[gn])
        for (nm, c0, c1, key) in wgrp[1:]:
            P.dma_group("pool", ("w", key), [(w_in[:, k, c0:c1], gla_w_in[li, k * 128:(k + 1) * 128, c0:c1])
                                             for k in range(8)], writes=[wB[nm]], max_dma_last_dim=4096)
        for i in range(8):
            P.memset("pool", S32[i][:], 0.0, [S32[i]])
            P.memset("pool", S16[i][:], 0.0, [S16[i]])
        P.memset("pool", gkl[:], 1.0, [gkl])

        a_rot = [0]
        tmp_bufs = [[Buf("tmp%d_%d" % (i, h)) for h in range(4)] for i in range(2)]

        def nextA():
            a_rot[0] ^= 1
            return A[a_rot[0]]

        def proj(hT, col0, n, dst):
            for k in range(8):
                P.mm(dst[:, 0:n], hT[:, k, :], w_in[:, k, col0:col0 + n], k == 0, k == 7, [hT, wbuf(col0)], [dst])

        def s0(c):
            P.dma("sp", ("hx", c % 2), hx[c % 2][:], h_src[c * 128:(c + 1) * 128, :],
                  reads=[h_bufs[c]], writes=[hx[c % 2]])

        def s1(c):
            hc, m, rs, hn_ = hx[c % 2], ms[c % 2], rstd[c % 2], hn[c % 2]
            P.act(hn_[:], hc[:], AF.Square, [hc], [hn_, m], accum_out=m[:, 0:1])
            rstd_from_ms(P, m[:, 0:1], rs[:, 0:1], D, [m], [rs], eng="act")
            P.stt(hn_[:], hc[:], rs[:, 0:1], g_mix[:], ALU.mult, ALU.mult, [hc, rs, g_mix], [hn_])

        def s2(c):
            hn_, hT = hn[c % 2], hnT[c % 3]
            for k in range(8):
                P.tr(Tb[:, k * 128:(k + 1) * 128], hn_[:, k * 128:(k + 1) * 128], ident[:], [hn_, ident], [Tb],
                     signal=(k == 7))
            P.copy("dve", hT[:].rearrange("p k n -> p (k n)"), Tb[:], [Tb], [hT])

        def s3a_1(c):
            hT = hnT[c % 3]
            for k in range(8):
                P.mm(M[:, 0:16], hT[:, k, :], w_in[:, k, 6144:6160], k == 0, k == 7, [hT, wB["gk"]], [M])
            P.copy("act", gkl[:, 0:16], M[:, 0:16], [M], [gkl])

        def s3a_2(c):
            P.tr(Tb[0:32, 0:128], gkl[:], ident[:], [gkl, ident], [Tb])
            P.copy("act", gklT[:], Tb[0:32, 0:128], [Tb], [gklT])

        def s3a_3(c):
            sp_ = sp[c % 2]
            for nb in range(2):
                P.mm(Gk[nb][:], gklT[0:17, :], wgk[0:17, nb * 512:(nb + 1) * 512], True, True, [gklT, wgk], [Gk[nb]])
                P.act(sp_[:, nb * 512:(nb + 1) * 512], Gk[nb][:], AF.Exp, [Gk[nb]], [sp_], scale=-1.0)
            P.act(sp_[:], sp_[:], AF.Ln, [sp_], [sp_], bias=1.0)
            P.copy("act", sph[:], sp_[:], [sp_], [sph])
            P.tt("pool", spl[:], sp_[:], sph[:], ALU.subtract, [sp_, sph], [spl])

        def s3b(c):
            sp_, E0, E1, E2, eb = sp[c % 2], E[c % 2][0], E[c % 2][1], E[c % 2][2], ebl[c % 2]
            U16 = mask2[:, 0, :]
            for nb in range(2):
                P.mm(Gk[nb][:], U16, sph[:, nb * 512:(nb + 1) * 512], True, False, [mask2, sph], [Gk[nb]], signal=False)
                P.mm(Gk[nb][:], U16, spl[:, nb * 512:(nb + 1) * 512], False, True, [mask2, spl], [Gk[nb]])
            for f in range(8):
                P.mm(M[:, 2 * f:2 * f + 2], sph[:, f * 128:(f + 1) * 128], ones16[:, 0:2], True, False,
                     [sph, ones16], [M], signal=False)
                P.mm(M[:, 2 * f:2 * f + 2], spl[:, f * 128:(f + 1) * 128], ones16[:, 0:2], False, True,
                     [spl, ones16], [M], signal=(f == 7))
            for nb in range(2):
                P.act(E0[:, nb * 512:(nb + 1) * 512], Gk[nb][:], AF.Exp, [Gk[nb]], [E0], scale=-1.0 / 16)
                P.act(E1[:, nb * 512:(nb + 1) * 512], Gk[nb][:], AF.Exp, [Gk[nb]], [E1], scale=1.0 / 16)
            P.act(eb[:], M[:, 0:16].rearrange("p (f t) -> p f t", t=2)[:, :, 0], AF.Exp, [M], [eb], scale=-1.0 / 16)
            for nb in range(2):
                P.mm(Gk[nb][:], L16[:], sph[:, nb * 512:(nb + 1) * 512], True, False, [L16, sph], [Gk[nb]], signal=False)
                P.mm(Gk[nb][:], L16[:], spl[:, nb * 512:(nb + 1) * 512], False, True, [L16, spl], [Gk[nb]])
                P.act(E2[:, nb * 512:(nb + 1) * 512], Gk[nb][:], AF.Exp, [Gk[nb]], [E2], scale=-1.0 / 16)

        def s4_q(c):
            hT, E0 = hnT[c % 3], E[c % 2][0]
            for nb in range(2):
                a = nextA()
                proj(hT, nb * 512, 512, a)
                P.stt(qt[:, nb * 512:(nb + 1) * 512], a[:], 256 ** -0.5, E0[:, nb * 512:(nb + 1) * 512],
                      ALU.mult, ALU.mult, [a, E0], [qt])

        def s4_k(c):
            hT, E1, E2 = hnT[c % 3], E[c % 2][1], E[c % 2][2]
            for nb in range(2):
                a = nextA()
                proj(hT, 1024 + nb * 512, 512, a)
                P.tt("dve", kt[:, nb * 512:(nb + 1) * 512], a[:], E1[:, nb * 512:(nb + 1) * 512], ALU.mult,
                     [a, E1], [kt])
                P.tt("dve", kd[:, nb * 512:(nb + 1) * 512], a[:], E2[:, nb * 512:(nb + 1) * 512], ALU.mult,
                     [a, E2], [kd])

        def s4_gate(c):
            hT = hnT[c % 3]
            for nb in range(4):
                a = nextA()
                proj(hT, 4096 + nb * 512, 512, a)
                P.act(gsil[:, nb * 512:(nb + 1) * 512], a[:], AF.Silu, [a], [gsil])
            for nb in range(4):
                P.tt("pool", gsil[:, nb * 512:(nb + 1) * 512], gsil[:, nb * 512:(nb + 1) * 512], gn[:], ALU.mult,
                     [gsil, gn], [gsil])

        def s4_v(c):
            hT = hnT[c % 3]
            for nb in range(4):
                a = nextA()
                proj(hT, 2048 + nb * 512, 512, a)
                P.copy("dve", v_sb[:, nb * 512:(nb + 1) * 512], a[:], [a], [v_sb])

        def s4_tr(c):
            for k in range(8):
                P.tr(Tb[:, k * 128:(k + 1) * 128], qt[:, k * 128:(k + 1) * 128], ident[:], [qt, ident], [Tb],
                     signal=(k == 7))
            P.copy("act", qtT[:].rearrange("p k n -> p (k n)"), Tb[:], [Tb], [qtT])
            for k in range(8):
                P.tr(Tb[:, k * 128:(k + 1) * 128], kt[:, k * 128:(k + 1) * 128], ident[:], [kt, ident], [Tb],
                     signal=(k == 7))
            P.copy("dve", ktT[:].rearrange("p k n -> p (k n)"), Tb[:], [Tb], [ktT])

        def s4_rec(c):
            eb = ebl[c % 2]
            for hh in range(4):
                for dt in range(2):
                    P.mm(M[:, hh * 128:(hh + 1) * 128], ktT[:, 2 * hh + dt, :], qtT[:, 2 * hh + dt, :],
                         dt == 0, dt == 1, [ktT, qtT], [M], signal=(hh == 3 and dt == 1))
            P.tt("dve", scT[:], M[:], mask2[:].rearrange("p r n -> p (r n)"), ALU.mult, [M, mask2], [scT])
            osb = on_sb[c % 2]
            KV = [A[0], A[1], Gk[0], Gk[1]]
            E1, E2 = E[c % 2][1], E[c % 2][2]
            tmpB = tmp_bufs[c % 2]
            for hh in range(4):
                o = O[hh % 2]
                tmpT = E1 if hh < 2 else E2
                tmp = tmpT[:, (hh % 2) * 512:(hh % 2 + 1) * 512]
                for dt in range(2):
                    P.mm(o[:], qtT[:, 2 * hh + dt, :], S16[2 * hh + dt][:], dt == 0, False,
                         [qtT, S16[2 * hh + dt]], [o], signal=False)
                P.mm(o[:], scT[:, hh * 128:(hh + 1) * 128], v_sb[:, hh * 512:(hh + 1) * 512], False, True,
                     [scT, v_sb], [o])
                P.tt("dve", tmp, o[:], gsil[:, hh * 512:(hh + 1) * 512], ALU.mult, [o, gsil], [tmpB[hh]])
                P.act(kt[:, 0:512], o[:], AF.Square, [o], [kt, mso], accum_out=mso[:, hh:hh + 1])
                rstd_from_ms(P, mso[:, hh:hh + 1], rso[:, hh:hh + 1], 512, [mso], [rso], eng="act")
                for dt in range(2):
                    i = 2 * hh + dt
                    a = KV[i % 4]
                    P.mm(a[:], kd[:, i * 128:(i + 1) * 128], v_sb[:, hh * 512:(hh + 1) * 512], True, True,
                         [kd, v_sb], [a])
                    P.stt(S32[i][:], S32[i][:], eb[:, i:i + 1], a[:], ALU.mult, ALU.add, [S32[i], eb, a], [S32[i]])
                    P.copy("pool", S16[i][:], S32[i][:], [S32[i]], [S16[i]])
                P.act(osb[:, hh * 512:(hh + 1) * 512], tmp, AF.Copy, [tmpB[hh], tmpT, rso], [osb],
                      scale=rso[:, hh:hh + 1])
            P.dma("pool", ("on", c % 2), on_d[c * 128:(c + 1) * 128, :], osb[:], reads=[osb], writes=[on_bufs[c]])

        for step in range(NCH + 4):
            c3, c4 = step - 3, step - 4
            v3, v4 = 0 <= c3 < NCH, 0 <= c4 < NCH
            if v3:
                s3a_1(c3)
            if v4:
                s4_gate(c4)
            if v3:
                s3a_2(c3)
            if v4:
                s4_v(c4)
            if v3:
                s3a_3(c3)
            if v4:
                s4_q(c4)
                s4_k(c4)
            if v3:
                s3b(c3)
            if v4:
                s4_tr(c4)
                s4_rec(c4)
            for k, fn in ((2, s2), (1, s1), (0, s0)):
                c = step - k
                if 0 <= c < NCH:
                    fn(c)
        Z.end()

    def phase_b(li, h_src, w_out_d, final):
        Z = Phase(nc, P, "b%d" % li)
        w_out = Z.sb("w_out", [128, 16, D], BF16)
        wg = Z.sb("wg", [128, 8, D], BF16)
        wp = Z.sb("wp", [128, 2, D], BF16)
        g_ple = Z.sb("g_ple", [128, D], F32)
        hx = [Z.sb("hx%d" % i, [128, D], F32) for i in range(2)]
        on_sb = [Z.sb("on%d" % i, [128, MIX], BF16) for i in range(2)]
        p_sb = [Z.sb("p%d" % i, [128, PLE], BF16) for i in range(2)]
        onTa = [Z.sb("onTa%d" % i, [128, 8, 128], BF16) for i in range(2)]
        onTb = [Z.sb("onTb%d" % i, [128, 8, 128], BF16) for i in range(2)]
        pT = [Z.sb("pT%d" % i, [128, 2, 128], BF16) for i in range(2)]
        junk = Z.sb("junk", [128, D], BF16)
        h1 = [Z.sb("h1_%d" % i, [128, D], F32) for i in range(3)]
        hn2 = [Z.sb("hn2_%d" % i, [128, D], BF16) for i in range(2)]
        hn2T = [Z.sb("hn2T%d" % i, [128, 8, 128], BF16) for i in range(2)]
        sg = Z.sb("sg", [128, D], F32)
        h2 = [Z.sb("h2_%d" % i, [128, D], F32) for i in range(2)]
        ms = [Z.sb("ms%d" % i, [128, 2], F32) for i in range(4)]
        rstd = [Z.sb("rstd%d" % i, [128, 2], F32) for i in range(4)]
        T0 = Z.ps("T0", [128, 1024], BF16)
        T1 = Z.ps("T1", [128, 1024], BF16)
        T2 = Z.ps("T2", [128, 1024], BF16)
        Y = [Z.ps("Y%d" % i, [128, 512]) for i in range(2)]
        R = [Z.ps("R%d" % i, [128, 512]) for i in range(3)]
        if final:
            g_fin = Z.sb("g_fin", [128, D], F32)
            o_sb = [Z.sb("o%d" % i, [128, D], F32) for i in range(2)]

        woB = [Buf("w_out_a"), Buf("w_out_b")]
        for nb_ in range(2):
            P.dma_group("pool", ("w", 0 if nb_ == 0 else 7),
                        [(w_out[:, k, nb_ * 512:(nb_ + 1) * 512], w_out_d[k * 128:(k + 1) * 128, nb_ * 512:(nb_ + 1) * 512])
                         for k in range(16)], writes=[woB[nb_]])
        wgB = [Buf("wg_a"), Buf("wg_b")]
        for nb_ in range(2):
            P.dma_group("pool", ("w", 1 if nb_ == 0 else 5),
                        [(wg[:, k, nb_ * 512:(nb_ + 1) * 512], ple_w_gate[li, k * 128:(k + 1) * 128, nb_ * 512:(nb_ + 1) * 512])
                         for k in range(8)], writes=[wgB[nb_]])
        P.dma_group("pool", ("w", 2), [(wp[:, k, :], ple_w_proj[li, k * 128:(k + 1) * 128, :]) for k in range(2)],
                    writes=[wp])
        P.dma("sp", ("w", 3), g_ple[:], ple_norm[li].partition_broadcast(128), writes=[g_ple])
        if final:
            P.dma("sp", ("w", 6), g_fin[:], final_norm.partition_broadcast(128), writes=[g_fin])

        r_rot = [0]

        def nextR():
            r_rot[0] = (r_rot[0] + 1) % 3
            return R[r_rot[0]]

        def s0(c):
            s = c % 2
            P.dma("sp", ("onl", s), on_sb[s][:], on_d[c * 128:(c + 1) * 128, :], reads=[on_bufs[c]], writes=[on_sb[s]])

        def s1(c):
            s = c % 2
            oc = on_sb[s]
            P.dma("sp", ("hx", s), hx[s][:], h_src[c * 128:(c + 1) * 128, :], reads=[h_bufs[c]], writes=[hx[s]])
            for k in range(16):
                T = T0 if k < 8 else T1
                P.tr(T[:, (k % 8) * 128:(k % 8 + 1) * 128], oc[:, k * 128:(k + 1) * 128], ident[:], [oc, ident], [T],
                     signal=(k % 8 == 7))
            P.copy("dve", onTa[s][:].rearrange("p k n -> p (k n)"), T0[:], [T0], [onTa[s]])
            P.copy("act", onTb[s][:].rearrange("p k n -> p (k n)"), T1[:], [T1], [onTb[s]])

        def s2(c):
            s = c % 2
            hc = hx[s]
            h1c = h1[c % 3]
            m, rs = ms[c % 4], rstd[c % 4]
            P.dma("pool", ("pl", s), p_sb[s][:], p_in[li, c * 128:(c + 1) * 128, :], writes=[p_sb[s]])
            for nb in range(2):
                y = Y[nb]
                for k in range(16):
                    oT = onTa[s] if k < 8 else onTb[s]
                    P.mm(y[:], oT[:, k % 8, :], w_out[:, k, nb * 512:(nb + 1) * 512], k == 0, k == 15,
                         [oT, woB[nb]], [y])
                P.tt("dve", h1c[:, nb * 512:(nb + 1) * 512], y[:], hc[:, nb * 512:(nb + 1) * 512], ALU.add,
                     [y, hc], [h1c])
            P.act(junk[:], h1c[:], AF.Square, [h1c], [junk, m], accum_out=m[:, 0:1])
            rstd_from_ms(P, m[:, 0:1], rs[:, 0:1], D, [m], [rs])
            P.stt(hn2[s][:], h1c[:], rs[:, 0:1], g_ple[:], ALU.mult, ALU.mult, [h1c, rs, g_ple], [hn2[s]])

        def s3(c):
            s = c % 2
            for k in range(8):
                P.tr(T2[:, k * 128:(k + 1) * 128], hn2[s][:, k * 128:(k + 1) * 128], ident[:], [hn2[s], ident], [T2],
                     signal=(k == 7))
            P.copy("dve", hn2T[s][:].rearrange("p k n -> p (k n)"), T2[:], [T2], [hn2T[s]])
            pc = p_sb[s]
            for k in range(2):
                P.tr(T0[:, k * 128:(k + 1) * 128], pc[:, k * 128:(k + 1) * 128], ident[:], [pc, ident], [T0],
                     signal=(k == 1))
            P.copy("act", pT[s][:].rearrange("p k n -> p (k n)"), T0[:, 0:256], [T0], [pT[s]])

        def s4(c):
            s = c % 2
            h1c = h1[c % 3]
            hout = h2[s]
            m, rs = ms[c % 4], rstd[c % 4]
            for nb in range(2):
                gb = nextR()
                for k in range(8):
                    P.mm(gb[:], hn2T[s][:, k, :], wg[:, k, nb * 512:(nb + 1) * 512], k == 0, k == 7,
                         [hn2T[s], wgB[nb]], [gb])
                P.act(sg[:, nb * 512:(nb + 1) * 512], gb[:], AF.Sigmoid, [gb], [sg])
                pb = nextR()
                for k in range(2):
                    P.mm(pb[:], pT[s][:, k, :], wp[:, k, nb * 512:(nb + 1) * 512], k == 0, k == 1, [pT[s], wp], [pb])
                P.tt("dve", sg[:, nb * 512:(nb + 1) * 512], pb[:], sg[:, nb * 512:(nb + 1) * 512], ALU.mult,
                     [pb, sg], [sg])
                P.tt("dve", hout[:, nb * 512:(nb + 1) * 512], sg[:, nb * 512:(nb + 1) * 512],
                     h1c[:, nb * 512:(nb + 1) * 512], ALU.add, [sg, h1c], [hout])
            if final:
                P.act(junk[:], hout[:], AF.Square, [hout], [junk, m], accum_out=m[:, 1:2])
                rstd_from_ms(P, m[:, 1:2], rs[:, 1:2], D, [m], [rs])
                P.stt(o_sb[s][:], hout[:], rs[:, 1:2], g_fin[:], ALU.mult, ALU.mult, [hout, rs, g_fin], [o_sb[s]])
                P.dma("sp", ("os", s), out[c * 128:(c + 1) * 128, :], o_sb[s][:], reads=[o_sb[s]], writes=[out_bufs[c]])
            else:
                P.dma("sp", ("hs", s), hbuf[c * 128:(c + 1) * 128, :], hout[:], reads=[hout], writes=[h_bufs[c]])

        run_pipeline([s0, s1, s2, s3, s4], NCH)
        Z.end()

    def phase_kv():
        Z = Phase(nc, P, "kv")
        wkv = Z.sb("wkv", [128, 8, 4096], BF16)
        g_kv = Z.sb("g_kv", [128, D], F32)
        hx = [Z.sb("hx%d" % i, [128, D], F32) for i in range(2)]
        junk = Z.sb("junk", [128, D], BF16)
        kvn = [Z.sb("kvn%d" % i, [128, D], BF16) for i in range(2)]
        kvnT = [Z.sb("kvnT%d" % i, [128, 8, 128], BF16) for i in range(2)]
        kT_sb = [Z.sb("kT%d" % i, [128, 16, 128], BF16) for i in range(2)]
        v_o = [Z.sb("vo%d" % i, [128, MIX], BF16) for i in range(2)]
        ms = [Z.sb("ms%d" % i, [128, 2], F32) for i in range(4)]
        rstd = [Z.sb("rstd%d" % i, [128, 2], F32) for i in range(4)]
        T0 = Z.ps("T0", [128, 1024], BF16)
        Bk = [Z.ps("B%d" % i, [128, 512]) for i in range(6)]
        wkvK, wkvV = Buf("wkvK"), Buf("wkvV")
        P.dma_group("pool", ("w", 4), [(wkv[:, k, 0:2048], w_kv[k * 128:(k + 1) * 128, 0:2048]) for k in range(8)],
                    writes=[wkvK], max_dma_last_dim=4096)
        P.dma_group("pool", ("w", 6), [(wkv[:, k, 2048:4096], w_kv[k * 128:(k + 1) * 128, 2048:4096])
                                       for k in range(8)], writes=[wkvV], max_dma_last_dim=4096)
        P.dma("sp", ("w", 5), g_kv[:], kv_norm.partition_broadcast(128), writes=[g_kv])
        b_rot = [0]

        def nextB():
            b_rot[0] = (b_rot[0] + 1) % 6
            return Bk[b_rot[0]]

        def s0(c):
            s = c % 2
            P.dma("sp", ("hx", s), hx[s][:], hbuf[c * 128:(c + 1) * 128, :], reads=[h_bufs[c]], writes=[hx[s]])

        def s1(c):
            s = c % 2
            m, rs = ms[c % 4], rstd[c % 4]
            P.act(junk[:], hx[s][:], AF.Square, [hx[s]], [junk, m], accum_out=m[:, 0:1])
            rstd_from_ms(P, m[:, 0:1], rs[:, 0:1], D, [m], [rs])
            P.stt(kvn[s][:], hx[s][:], rs[:, 0:1], g_kv[:], ALU.mult, ALU.mult, [hx[s], rs, g_kv], [kvn[s]])

        def s2(c):
            s = c % 2
            for k in range(8):
                P.tr(T0[:, k * 128:(k + 1) * 128], kvn[s][:, k * 128:(k + 1) * 128], ident[:], [kvn[s], ident], [T0],
                     signal=(k == 7))
            P.copy("dve", kvnT[s][:].rearrange("p k n -> p (k n)"), T0[:], [T0], [kvnT[s]])

        def s3(c):
            s = c % 2
            kts = kT_sb[s]
            for cg in range(4):
                kb = nextB()
                for ci in range(4):
                    ct = cg * 4 + ci
                    for k in range(8):
                        P.mm(kb[:, ci * 128:(ci + 1) * 128], wkv[:, k, ct * 128:(ct + 1) * 128], kvnT[s][:, k, :],
                             k == 0, k == 7, [wkvK, kvnT[s]], [kb], signal=(k == 7 and ci == 3))
                P.copy("act" if cg % 2 else "dve", kts[:, cg * 4:(cg + 1) * 4, :].rearrange("p k n -> p (k n)"),
                       kb[:], [kb], [kts])
            P.dma("pool", ("ks", s), kT_d[:, :, c * 128:(c + 1) * 128].rearrange("t d n -> d t n"), kts[:],
                  reads=[kts], writes=[kv_buf])
            vo = v_o[s]
            for nb in range(4):
                vb = nextB()
                for k in range(8):
                    P.mm(vb[:], kvnT[s][:, k, :], wkv[:, k, 2048 + nb * 512:2048 + (nb + 1) * 512], k == 0, k == 7,
                         [kvnT[s], wkvV], [vb])
                P.copy("act" if nb % 2 else "dve", vo[:, nb * 512:(nb + 1) * 512], vb[:], [vb], [vo])
            P.dma("pool", ("vs", s), v_d[c * 128:(c + 1) * 128, :], vo[:], reads=[vo], writes=[kv_buf])

        run_pipeline([s0, s1, s2, s3], NCH)
        Z.end()

    def phase_diff(li):
        j = li - 2
        lam_init = 0.8 - 0.6 * math.exp(-0.3 * li)
        Z = Phase(nc, P, "d%d" % li)
        hnT = Z.sb("hnT", [128, 8, S], BF16)
        g_mix = Z.sb("g_mix", [128, D], F32)
        gd = Z.sb("gd", [128, 256], F32)
        hx = [Z.sb("hx%d" % i, [128, D], F32) for i in range(2)]
        ms = Z.sb("ms", [128, 4], F32)
        rstd = Z.sb("rstd", [128, 4], F32)
        lam4 = Z.sb("lam4", [128, 4], F32)
        prods = Z.sb("prods", [128, 2], F32)
        lam = Z.sb("lam", [128, 4], F32)
        wq = [Z.sb("wq%d" % i, [128, 8, 256], BF16) for i in range(2)]
        wgt = [Z.sb("wgt%d" % i, [128, 8, 256], BF16) for i in range(2)]
        kTh = [Z.sb("kTh%d" % i, [128, 2, S], BF16) for i in range(2)]
        vh = [Z.sb("vh%d" % i, [128, NCH, 258], BF16) for i in range(2)]
        qTh = Z.sb("qTh", [128, 2, S], BF16)
        PT = [Z.sb("PT%d" % i, [128, 512], BF16) for i in range(4)]
        o_sb = [Z.sb("o%d" % i, [128, 256], F32) for i in range(2)]
        oraw = [Z.sb("or%d" % q, [128, 2, 258], F32) for q in range(2)]
        sil_h = Z.sb("sil_h", [128, NCH, 256], BF16)
        on_o = [Z.sb("ono%d" % i, [128, 256], BF16) for i in range(2)]
        rr = Z.sb("rr", [128, 4], F32)
        Sb = [Z.ps("S%d" % i, [128, 512]) for i in range(4)]
        T0 = TB(Sb[3].t.bitcast(BF16), "T0")
        T0.b = Sb[3].b
        Oa = [[Z.ps("O%d%d" % (q, t), [128, 512]) for t in range(2)] for q in range(2)]
        Gb = Sb[2]

        P.dma("sp", ("w", 0), g_mix[:], norm_mix[li].partition_broadcast(128), writes=[g_mix])
        P.dma("sp", ("w", 1), gd[:], diff_norm[j].partition_broadcast(128), writes=[gd])
        P.ts("dve", gd[:], gd[:], 1.0 - lam_init, None, ALU.mult, None, [gd], [gd])
        P.dma("sp", ("w", 2), lam4[:], diff_lambda[j].rearrange("f d -> d f"), writes=[lam4],
              allow_slow_non_contiguous=True)
        P.tt("dve", prods[:, 0:1], lam4[:, 0:1], lam4[:, 1:2], ALU.mult, [lam4], [prods])
        P.tt("dve", prods[:, 1:2], lam4[:, 2:3], lam4[:, 3:4], ALU.mult, [lam4], [prods])
        P.mm(Gb[:, 0:2], ones32[:], prods[:], True, True, [ones32, prods], [Gb])
        P.act(lam[:, 0:2], Gb[:, 0:2], AF.Exp, [Gb], [lam])
        P.tt("dve", lam[:, 2:3], lam[:, 1:2], lam[:, 0:1], ALU.subtract, [lam], [lam])
        P.ts("dve", lam[:, 2:3], lam[:, 2:3], -lam_init, None, ALU.add, None, [lam], [lam])

        for i in range(2):
            P.memset("pool", vh[i][:, :, 256:258], 1.0, [vh[i]])

        def head_loads(hd):
            s = hd % 2
            P.dma_group("pool", ("wq", s), [(wq[s][:, k, :],
                                             diff_w_in[j, k * 128:(k + 1) * 128, hd * 256:(hd + 1) * 256])
                                            for k in range(8)], writes=[wq[s]])
            P.dma_group("pool", ("wg", s), [(wgt[s][:, k, :],
                                             diff_w_in[j, k * 128:(k + 1) * 128, 2048 + hd * 256:2048 + (hd + 1) * 256])
                                            for k in range(8)], writes=[wgt[s]])
            P.dma("sp", ("kh", s), kTh[s][:], kT_d[2 * hd:2 * hd + 2].rearrange("t d n -> d t n"),
                  reads=[kv_buf], writes=[kTh[s]])
            P.dma_group("sp", ("vh", s), [(vh[s][:, g * 8:(g + 1) * 8, 0:256],
                                           v_d[g * 1024:(g + 1) * 1024, hd * 256:(hd + 1) * 256].rearrange(
                                               "(n p) c -> p n c", p=128)) for g in range(4)],
                        reads=[kv_buf], writes=[vh[s]])

        head_loads(0)
        msP = [Z.sb("msP%d" % i, [128, 2], F32) for i in range(2)]
        rsP = [Z.sb("rsP%d" % i, [128, 2], F32) for i in range(2)]
        hnS = [qTh[:, i, 0:1024] for i in range(2)]
        hnB = [Buf("hnS0"), Buf("hnS1")]
        jnk = qTh[:, 0, 2048:3072]
        jnkB = Buf("jnk")

        def p0(c):
            P.dma("sp", ("hx", c % 2), hx[c % 2][:], hbuf[c * 128:(c + 1) * 128, :],
                  reads=[h_bufs[c]], writes=[hx[c % 2]])

        def p1(c):
            hc, m, rs = hx[c % 2], msP[c % 2], rsP[c % 2]
            P.act(jnk, hc[:], AF.Square, [hc], [jnkB, m], accum_out=m[:, 0:1])
            rstd_from_ms(P, m[:, 0:1], rs[:, 0:1], D, [m], [rs])
            P.stt(hnS[c % 2], hc[:], rs[:, 0:1], g_mix[:], ALU.mult, ALU.mult, [hc, rs, g_mix], [hnB[c % 2]])

        def p2(c):
            src = hnS[c % 2]
            for k in range(8):
                P.tr(T0[:, k * 128:(k + 1) * 128], src[:, k * 128:(k + 1) * 128], ident[:], [hnB[c % 2], ident], [T0],
                     signal=(k == 7))
            P.copy("dve", hnT[:, :, c * 128:(c + 1) * 128], T0[:].rearrange("p (k n) -> p k n", k=8), [T0], [hnT])

        run_pipeline([p0, p1, p2], NCH)

        scale = 128 ** -0.5
        LA = 3
        silB = [Buf("sil%d" % i) for i in range(16)]
        for hd in range(8):
            s = hd % 2
            if hd + 1 < 8:
                head_loads(hd + 1)
            for t in range(2):
                for g in range(8):
                    sb_ = Sb[g % 3]
                    for k in range(8):
                        P.mm(sb_[:], wq[s][:, k, t * 128:(t + 1) * 128], hnT[:, k, g * 512:(g + 1) * 512],
                             k == 0, k == 7, [wq[s], hnT], [sb_])
                    ev = "act" if (g % 2 or (t == 0 and g < 4)) else "dve"
                    P.copy(ev, qTh[:, t, g * 512:(g + 1) * 512], sb_[:], [sb_], [qTh])
            for pr in range(16):
                sb_ = Sb[pr % 3]
                for q in range(2):
                    qt_i = 2 * pr + q
                    for k in range(8):
                        P.mm(sb_[:, q * 256:(q + 1) * 256], hnT[:, k, qt_i * 128:(qt_i + 1) * 128], wgt[s][:, k, :],
                             k == 0, k == 7, [hnT, wgt[s]], [sb_], signal=(k == 7 and q == 1))
                P.act(sil_h[:, 2 * pr:2 * pr + 2, :].rearrange("p a c -> p (a c)"), sb_[:], AF.Silu, [sb_], [silB[pr]])
                for q in range(2):
                    P.tt("pool", sil_h[:, 2 * pr + q, :], sil_h[:, 2 * pr + q, :], gd[:], ALU.mult,
                         [silB[pr], gd], [silB[pr]])

            items = [(r, kt_) for r in range(16) for kt_ in range(2 * r + 2)]

            def emit_ST(i):
                r, kt_ = items[i]
                sb_ = Sb[i % 4]
                if kt_ == 2 * r + 1:
                    for t in range(2):
                        reg = sb_[:, t * 256 + 128:(t + 1) * 256]
                        P.mm(reg, kTh[s][:, t, kt_ * 128:(kt_ + 1) * 128], qTh[:, t, r * 256 + 128:(r + 1) * 256],
                             True, False, [kTh[s], qTh], [sb_], signal=False)
                        P.mm(reg, ident[:], maskb[:], False, True, [ident, maskb], [sb_], signal=(t == 1))
                else:
                    for t in range(2):
                        dg = (kt_ == 2 * r)
                        P.mm(sb_[:, t * 256:(t + 1) * 256], kTh[s][:, t, kt_ * 128:(kt_ + 1) * 128],
                             qTh[:, t, r * 256:(r + 1) * 256], True, not dg, [kTh[s], qTh], [sb_],
                             signal=(t == 1 and not dg))
                        if dg:
                            P.mm(sb_[:, t * 256:t * 256 + 128], ident[:], maskb[:], False, True, [ident, maskb], [sb_],
                                 signal=(t == 1))

            def emit_exp(i):
                r, kt_ = items[i]
                sb_ = Sb[i % 4]
                pt = PT[i % 4]
                if kt_ != 2 * r + 1:
                    P.act(pt[:], sb_[:], AF.Exp, [sb_], [pt], scale=scale)
                else:
                    v3 = sb_[:].rearrange("p (t q) -> p t q", t=2)[:, :, 128:256]
                    p3 = pt[:].rearrange("p (t q) -> p t q", t=2)[:, :, 128:256]
                    P.act(p3, v3, AF.Exp, [sb_], [pt], scale=scale)

            def emit_PV(i):
                r, kt_ = items[i]
                pt = PT[i % 4]
                qtiles = (0, 1) if kt_ != 2 * r + 1 else (1,)
                for q in qtiles:
                    last = (kt_ == 2 * r + q)
                    for t in range(2):
                        P.mm(Oa[q][t][:, 0:257], pt[:, t * 256 + q * 128:t * 256 + (q + 1) * 128],
                             vh[s][:, kt_, 0:257], kt_ == 0, last, [pt, vh[s]], [Oa[q][t]],
                             signal=(last or t == 1))

            def fin_A1(r, q):
                O1, O2 = Oa[q][0], Oa[q][1]
                rw = oraw[q]
                P.copy("dve", rw[:, 0, 0:257], O1[:, 0:257], [O1], [rw])
                P.copy("dve", rw[:, 1, 0:257], O2[:, 0:257], [O2], [rw])

            def fin_A2(r, q):
                o_ = o_sb[q]
                rw = oraw[q]
                rq = rr[:, 2 * q:2 * q + 2]
                P.recip(rq, rw[:, :, 256], [rw], [rr])
                P.ts("dve", o_[:], rw[:, 1, 0:256], rr[:, 2 * q + 1:2 * q + 2], lam[:, 2:3], ALU.mult, ALU.mult,
                     [rw, rr, lam], [o_])
                P.stt(o_[:], rw[:, 0, 0:256], rr[:, 2 * q:2 * q + 1], o_[:], ALU.mult, ALU.add, [rw, rr, o_], [o_])
                P.op("dve", lambda h, o_=o_, q=q: h.scalar_tensor_tensor(
                    out=on_o[q][:], in0=o_[:], scalar=1.0, in1=o_[:], op0=ALU.mult, op1=ALU.mult,
                    accum_out=ms[:, 1 + q:2 + q]), [o_], [on_o[q], ms])

            def fin_B(r, q):
                qt_i = 2 * r + q
                o_ = o_sb[q]
                rstd_from_ms(P, ms[:, 1 + q:2 + q], rstd[:, 1 + q:2 + q], 256, [ms], [rstd])
                oo = on_o[q]
                P.stt(oo[:], o_[:], rstd[:, 1 + q:2 + q], sil_h[:, qt_i, :], ALU.mult, ALU.mult,
                      [o_, rstd, silB[qt_i // 2]], [oo])
                P.dma("pool", ("ono", q), on_d[qt_i * 128:(qt_i + 1) * 128, hd * 256:(hd + 1) * 256], oo[:],
                      reads=[oo], writes=[on_bufs[qt_i]])

            deferred = []
            n_it = len(items)
            for i in range(min(LA, n_it)):
                emit_ST(i)
            for i in range(n_it):
                r, kt_ = items[i]
                if i + LA < n_it:
                    emit_ST(i + LA)
                emit_exp(i)
                emit_PV(i)
                for (due, fn) in [d for d in deferred if d[0] <= i]:
                    fn()
                deferred = [d for d in deferred if d[0] > i]
                if kt_ == 2 * r:
                    fin_A1(r, 0)
                if kt_ == 2 * r + 1:
                    fin_A1(r, 1)
                    fin_A2(r, 0)
                    fin_A2(r, 1)
                    deferred.append((i + 2, lambda r=r: fin_B(r, 0)))
                    deferred.append((i + 2, lambda r=r: fin_B(r, 1)))
            for (due, fn) in deferred:
                fn()
        Z.end()

    P.barrier()
    for ph in phases:
        kind, li = ph[0], int(ph[1])
        if kind == "g":
            phase_gla(li, x if li == 0 else hbuf)
        elif kind == "b":
            wo = gla_w_out[li] if li < 2 else diff_w_out[li - 2]
            phase_b(li, x if li == 0 else hbuf, wo, final=(li == 3))
            if li == 1:
                phase_kv()
        elif kind == "d":
            phase_diff(li)
    G.es.close()
    P.close()
    return nc, P


_CACHE = {}


def _in_maps(inputs):
    maps = []
    shared = {k: np.ascontiguousarray(v, dtype=np.float32) for k, v in inputs.items() if k not in ("x", "p")}
    for b in range(N_CORES):
        m = dict(shared)
        m["x"] = np.ascontiguousarray(inputs["x"][b], dtype=np.float32)
        m["p"] = np.ascontiguousarray(inputs["p"][:, b], dtype=np.float32)
        maps.append(m)
    return maps


def kernel(**inputs):
    if "nc" not in _CACHE:
        _CACHE["nc"] = build_program()[0]
    nc = _CACHE["nc"]
    res = run_bass_kernel_spmd(nc, _in_maps(inputs), core_ids=list(range(N_CORES)))
    return np.stack([np.asarray(r["out"], dtype=np.float32) for r in res.results], axis=0)
```
